# Optimizing a Trainium2 kernel written in Bass

```python
import jax, jax.numpy as jnp
from jax import lax
import numpy as np

D_MODEL = 1024
BATCH = 32
SEQ = 2048
DEPTH = 1

FOX_HEADS = 8
FOX_HEAD_DIM = 64
FOX_WIDTH = FOX_HEADS * FOX_HEAD_DIM
MLA_HEADS = 8
MLA_NOPE_DIM = 64
MLA_ROPE_DIM = 32
MLA_QK_DIM = MLA_NOPE_DIM + MLA_ROPE_DIM
MLA_V_DIM = 64
MLA_Q_RANK = 256
MLA_KV_RANK = 128
MLA_WIDTH = MLA_HEADS * MLA_V_DIM
MIX_WIDTH = FOX_WIDTH + MLA_WIDTH
IN_SPLITS = (FOX_WIDTH, FOX_WIDTH, FOX_WIDTH, FOX_HEADS, MLA_Q_RANK, MLA_KV_RANK, MLA_ROPE_DIM)
IN_WIDTH = sum(IN_SPLITS)
D_FF = -(-8 * D_MODEL // (3 * 256)) * 256

Q_BLOCK = 128
ROPE_THETA = 10000.0
NORM_EPS = 1e-6

kernel_name = "hymba_fox_mla_hybrid_layer"


def rmsnorm(x, g):
    xf = x.astype(jnp.float32)
    y = xf * lax.rsqrt(jnp.mean(xf * xf, axis=-1, keepdims=True) + NORM_EPS)
    return (y * g.astype(jnp.float32)).astype(x.dtype)


def rope_tables(positions, dim):
    inv_freq = ROPE_THETA ** (-jnp.arange(0, dim, 2, dtype=jnp.float32) / dim)
    ang = positions.astype(jnp.float32)[:, :, None] * inv_freq[None, None, :]
    return jnp.cos(ang)[:, None], jnp.sin(ang)[:, None]


def apply_rope(x, cos, sin):
    xf = x.astype(jnp.float32)
    x1, x2 = jnp.split(xf, 2, axis=-1)
    return jnp.concatenate([x1 * cos - x2 * sin, x2 * cos + x1 * sin], axis=-1).astype(x.dtype)


def causal_block_attention(q, k, v, scale, log_decay=None):
    seq = q.shape[2]
    outs = []
    for blk in range(seq // Q_BLOCK):
        q0 = blk * Q_BLOCK
        q1 = q0 + Q_BLOCK
        qb = q[:, :, q0:q1]
        kb = k[:, :, :q1]
        vb = v[:, :, :q1]
        logits = jnp.einsum('bhqd,bhkd->bhqk', qb, kb,
                            preferred_element_type=jnp.float32) * scale
        if log_decay is not None:
            ld = log_decay.astype(jnp.float32)
            logits = logits + (ld[:, :, q0:q1, None] - ld[:, :, None, :q1])
        q_pos = q0 + jnp.arange(Q_BLOCK)
        k_pos = jnp.arange(q1)
        mask = k_pos[None, :] <= q_pos[:, None]
        logits = jnp.where(mask, logits, -jnp.inf)
        p = jax.nn.softmax(logits, axis=-1)
        outs.append(jnp.einsum('bhqk,bhkd->bhqd', p.astype(vb.dtype), vb))
    return jnp.concatenate(outs, axis=2)


def to_heads(t, n_heads):
    b, s, _ = t.shape
    return t.reshape(b, s, n_heads, -1).transpose(0, 2, 1, 3)


def from_heads(t):
    b, h, s, d = t.shape
    return t.transpose(0, 2, 1, 3).reshape(b, s, h * d)


def hybrid_mixer(h, cos, sin, w_in, b_fgate, q_norm_g, w_uq, kv_norm_g, w_ukv,
                 fox_out_g, mla_out_g, w_o):
    b, s, _ = h.shape
    proj = jnp.einsum('bsd,de->bse', h, w_in)
    cuts = list(np.cumsum(IN_SPLITS)[:-1])
    fq, fk, fv, f_logit, q_lat, kv_lat, k_rope = jnp.split(proj, cuts, axis=-1)

    log_f = jax.nn.log_sigmoid((f_logit + b_fgate).astype(jnp.float32))
    c = jnp.cumsum(log_f, axis=1).transpose(0, 2, 1)
    fox_o = causal_block_attention(to_heads(fq, FOX_HEADS), to_heads(fk, FOX_HEADS),
                                   to_heads(fv, FOX_HEADS), FOX_HEAD_DIM ** -0.5, c)
    fox_o = rmsnorm(from_heads(fox_o), fox_out_g)

    q = jnp.einsum('bsr,re->bse', rmsnorm(q_lat, q_norm_g), w_uq)
    q = to_heads(q, MLA_HEADS)
    q_nope, q_pe = q[..., :MLA_NOPE_DIM], q[..., MLA_NOPE_DIM:]
    q = jnp.concatenate([q_nope, apply_rope(q_pe, cos, sin)], axis=-1)
    kv = jnp.einsum('bsr,re->bse', rmsnorm(kv_lat, kv_norm_g), w_ukv)
    kv = to_heads(kv, MLA_HEADS)
    k_nope, v = kv[..., :MLA_NOPE_DIM], kv[..., MLA_NOPE_DIM:]
    k_pe = apply_rope(k_rope[:, None], cos, sin)
    k = jnp.concatenate([k_nope, jnp.broadcast_to(k_pe, (b, MLA_HEADS, s, MLA_ROPE_DIM))], axis=-1)
    mla_o = causal_block_attention(q, k, v, MLA_QK_DIM ** -0.5)
    mla_o = rmsnorm(from_heads(mla_o), mla_out_g)

    return jnp.einsum('bse,ed->bsd', jnp.concatenate([fox_o, mla_o], axis=-1), w_o)


def swiglu(h, w_gate, w_up, w_down):
    g = jnp.einsum('bsd,df->bsf', h, w_gate)
    u = jnp.einsum('bsd,df->bsf', h, w_up)
    return jnp.einsum('bsf,fd->bsd', jax.nn.silu(g) * u, w_down)


def setup_inputs(seed: int = 0) -> dict:
    key = jax.random.key(seed)
    ks = jax.random.split(key, 20)

    def w(k, shape, fan_in):
        return jax.random.normal(k, shape, jnp.float32) * fan_in ** -0.5

    def gain(k, shape):
        return 1.0 + 0.05 * jax.random.normal(k, shape, jnp.float32)

    x = jax.random.normal(ks[0], (BATCH, SEQ, D_MODEL), jnp.float32)
    offsets = jax.random.randint(ks[1], (BATCH, 1), 0, 4096, dtype=jnp.int32)
    positions = offsets + jnp.arange(SEQ, dtype=jnp.int32)[None, :]
    return {
        "x": x,
        "positions": positions,
        "norm_mix_g": gain(ks[2], (DEPTH, D_MODEL)),
        "w_in": w(ks[3], (DEPTH, D_MODEL, IN_WIDTH), D_MODEL),
        "b_fgate": 1.0 + 3.0 * jax.random.uniform(ks[4], (DEPTH, FOX_HEADS), jnp.float32),
        "q_norm_g": gain(ks[5], (DEPTH, MLA_Q_RANK)),
        "w_uq": w(ks[6], (DEPTH, MLA_Q_RANK, MLA_HEADS * MLA_QK_DIM), MLA_Q_RANK),
        "kv_norm_g": gain(ks[7], (DEPTH, MLA_KV_RANK)),
        "w_ukv": w(ks[8], (DEPTH, MLA_KV_RANK, MLA_HEADS * (MLA_NOPE_DIM + MLA_V_DIM)), MLA_KV_RANK),
        "fox_out_g": gain(ks[9], (DEPTH, FOX_WIDTH)),
        "mla_out_g": gain(ks[10], (DEPTH, MLA_WIDTH)),
        "w_o": w(ks[11], (DEPTH, MIX_WIDTH, D_MODEL), MIX_WIDTH),
        "norm_ffn_g": gain(ks[12], (DEPTH, D_MODEL)),
        "w_gate": w(ks[13], (DEPTH, D_MODEL, D_FF), D_MODEL),
        "w_up": w(ks[14], (DEPTH, D_MODEL, D_FF), D_MODEL),
        "w_down": w(ks[15], (DEPTH, D_FF, D_MODEL), D_FF),
        "final_norm_g": gain(ks[16], (D_MODEL,)),
    }


def reference(x, positions, norm_mix_g, w_in, b_fgate, q_norm_g, w_uq, kv_norm_g, w_ukv,
              fox_out_g, mla_out_g, w_o, norm_ffn_g, w_gate, w_up, w_down, final_norm_g):
    cos, sin = rope_tables(positions, MLA_ROPE_DIM)
    for l in range(DEPTH):
        h = rmsnorm(x, norm_mix_g[l])
        x = x + hybrid_mixer(h, cos, sin, w_in[l], b_fgate[l], q_norm_g[l], w_uq[l],
                             kv_norm_g[l], w_ukv[l], fox_out_g[l], mla_out_g[l], w_o[l])
        h = rmsnorm(x, norm_ffn_g[l])
        x = x + swiglu(h, w_gate[l], w_up[l], w_down[l])
    return rmsnorm(x, final_norm_g)
```

```python
import math
from contextlib import contextmanager
import numpy as np
import concourse.bass as bass
import concourse.mybir as mybir
from concourse.bass_utils import run_bass_kernel_spmd

F32 = mybir.dt.float32
BF16 = mybir.dt.bfloat16
I32 = mybir.dt.int32
AF = mybir.ActivationFunctionType
ALU = mybir.AluOpType

S = 2048
D = 1024
NT = S // 128
NCH = S // 512
DFF = 2816
NFC = DFF // 128
EPS = 1e-6
C_F = 1536
C_QL = 1544
C_KV = 1800
C_KR = 1928
WIN = 1992
TWO_PI = 2.0 * math.pi
CW1 = 6.28125
CW2 = TWO_PI - CW1


class Buf:
    registry = []

    def __init__(self, name, ap, excl=False, parent=None):
        self.name = name
        self.ap = ap
        self.excl = excl
        self.rng = parent.rng if parent is not None else None
        self.arena = parent.arena if parent is not None else None
        self.alias = []
        Buf.registry.append(self)
        self.wr = None
        self.rds = {}
        self.dsem = None
        self.dcount = 0


class Ctx:
    def __init__(self, nc):
        self.nc = nc
        self.eng = {"pe": nc.tensor, "act": nc.scalar, "dve": nc.vector, "pool": nc.gpsimd, "sp": nc.sync}
        self.psem = {k: nc.alloc_semaphore("prog_" + k) for k in ("pe", "act", "dve", "pool")}
        self.pcount = {k: 0 for k in self.psem}
        self.waited = {}
        self.dma_last = {}

    def wait(self, e, tok):
        if tok is None:
            return
        sem, val, key = tok
        if e == "pe" and key == "pe":
            return
        if self.waited.get((e, key), 0) >= val:
            return
        self.eng[e].wait_ge(sem, val)
        self.waited[(e, key)] = val

    def deps(self, e, reads, writes):
        for b in reads:
            self.wait(e, b.wr)
            if b.excl:
                for k, t in b.rds.items():
                    if k != e:
                        self.wait(e, t)
        for b in writes:
            self.wait(e, b.wr)
            for k, t in b.rds.items():
                self.wait(e, t)
            for a in b.alias:
                self.wait(e, a.wr)
                for k, t in a.rds.items():
                    self.wait(e, t)

    def done(self, e, inst, reads, writes):
        self.pcount[e] += 1
        inst.then_inc(self.psem[e], 1)
        tok = (self.psem[e], self.pcount[e], e)
        for b in reads:
            b.rds[e] = tok
        for b in writes:
            b.wr = tok
            b.rds = {}
        return tok

    def op(self, e, fn, reads=(), writes=()):
        self.deps(e, reads, writes)
        inst = fn(self.eng[e])
        return self.done(e, inst, reads, writes)

    @contextmanager
    def grp(self, e, reads=(), writes=()):
        self.deps(e, reads, writes)
        h = [None]
        yield h
        self.done(e, h[0], reads, writes)

    def dma(self, q, out, in_, reads=(), writes=(), semof=None):
        self.deps(q, reads, writes)
        sb = semof if semof is not None else writes[0]
        if sb.dsem is None:
            sb.dsem = self.nc.alloc_semaphore("d_" + sb.name)
        inst = self.eng[q].dma_start(out=out, in_=in_)
        sb.dcount += 16
        inst.then_inc(sb.dsem, 16)
        key = "d_" + sb.name
        tok = (sb.dsem, sb.dcount, key)
        self.dma_last[key] = tok
        for b in reads:
            b.rds[key] = tok
        for b in writes:
            b.wr = tok
            b.rds = {}
        return tok

    def barrier(self, dummy):
        for e in ("pe", "act", "dve"):
            if self.pcount[e]:
                self.wait("pool", (self.psem[e], self.pcount[e], e))
        for tok in self.dma_last.values():
            self.wait("pool", tok)
        tok = self.op("pool", lambda g: g.memset(dummy.ap, 0.0), writes=[dummy])
        for e in ("pe", "act", "dve", "sp"):
            self.wait(e, tok)


def build_nc(nseq):
    nc = bass.Bass("TRN2", target_bir_lowering=False)
    cx = Ctx(nc)
    T, A, V, G = nc.tensor, nc.scalar, nc.vector, nc.gpsimd

    def din(name, shape, dt=F32):
        return nc.dram_tensor(name, list(shape), dt, kind="ExternalInput").ap()

    x = din("x", [nseq, S, D])
    pos = din("positions", [nseq, S], I32)
    g_mix_d = din("norm_mix_g", [1, D])
    w_in_d = din("w_in", [D, 1960])
    bf_d = din("b_fgate", [8, 1])
    qg_d = din("q_norm_g", [256, 1])
    w_uq_d = din("w_uq", [256, 768])
    kvg_d = din("kv_norm_g", [128, 1])
    w_ukv_d = din("w_ukv", [128, 1024])
    g_fox_d = din("fox_out_g", [1, 512])
    g_mla_d = din("mla_out_g", [1, 512])
    w_o_d = din("w_o", [D, D])
    g_ffn_d = din("norm_ffn_g", [1, D])
    w_g_d = din("w_gate", [D, DFF])
    w_u_d = din("w_up", [D, DFF])
    w_d_d = din("w_down", [DFF, D])
    g_fin_d = din("final_norm_g", [1, D])
    freq_d = din("freq_tab", [128, 4])
    out = nc.dram_tensor("out", [nseq, S, D], F32, kind="ExternalOutput").ap()

    def dscr(name, shape):
        return nc.dram_tensor(name, list(shape), BF16, kind="Internal").ap()

    win_b = dscr("win_b", [D, WIN])
    wo_b = dscr("wo_b", [D, D])
    wgu_b = dscr("wgu_b", [NFC, 128, 2, 8, 128])
    wd_b = dscr("wd_b", [DFF, D])

    base0 = (nc.sbuf_base + 63) // 64 * 64
    top = nc.sbuf_top
    cur = [base0]

    Buf.registry = []
    arena = ["A"]

    def sb(name, shape, dt, excl=False):
        nbytes = int(np.prod(shape[1:])) * (4 if dt in (F32, I32) else 2)
        off = cur[0]
        cur[0] = (off + nbytes + 63) // 64 * 64
        assert cur[0] <= top, f"SBUF overflow at {name}: {cur[0]} > {top}"
        t = nc.alloc_sbuf_tensor_at(name, list(shape), dt, offset=off)
        b = Buf(name, t.ap())
        b.rng = (off, off + nbytes)
        b.arena = arena[0]
        return b

    O_all = sb("O_all", [128, NT, 1024], BF16)
    ident = sb("ident", [128, 128], BF16)
    maskT = sb("maskT", [128, 128], BF16)
    ones_bf = sb("ones_bf", [128, 128], BF16)
    wuq = sb("wuq", [128, 2, 768], BF16)
    wuqs = sb("wuqs", [128, 2, 8, 96], BF16)
    wukv = sb("wukv", [128, 1024], BF16)
    freq = sb("freq", [128, 4], F32)
    negb = sb("negb", [128, 1], F32)
    qg = sb("qg", [128, 2], F32)
    kvg = sb("kvg", [128, 1], F32)
    ones8 = sb("ones8", [128, 1], F32)
    dummy = sb("dummy", [128, 1], F32)
    stats = [sb(f"st{i}", [128, 2], F32) for i in range(8)]
    rrs = [sb(f"rr{i}", [128, 4], F32) for i in range(2)]
    rkv_tok = sb("rkv_tok", [128, NT], F32)
    sqj = [sb(f"sqj{i}", [128, 512], BF16) for i in range(2)]
    arenaBC = cur[0]

    arena[0] = "B"
    hT = sb("hT", [128, 8, S], BF16)
    hTc = [Buf(f"hTc{i}", hT.ap, parent=hT) for i in range(NCH)]
    xin = [sb(f"xin{i}", [128, 1024], F32) for i in range(3)]
    hbf = [sb(f"hbf{i}", [128, 1024], BF16) for i in range(2)]
    g_mix = sb("g_mix", [128, 1024], F32)
    w_in = sb("w_in", [128, 8, WIN], BF16)
    ropeS = sb("ropeS", [128, 1024], F32)
    w_qk = Buf("w_qk", w_in.ap, parent=w_in)
    w_vf = Buf("w_vf", w_in.ap, parent=w_in)
    w_lat = Buf("w_lat", w_in.ap, parent=w_in)
    w_kr = Buf("w_kr", w_in.ap, parent=w_in)
    TMP = sb("TMP", [128, S], F32)
    T1 = sb("T1", [128, S], F32)
    T2 = sb("T2", [128, S], F32)
    q_latT = sb("q_latT", [128, 2, S], BF16)
    kv_latT = sb("kv_latT", [128, S], BF16)
    k_peT = sb("k_peT", [128, S], BF16)
    QT = [sb(f"QT{i}", [128, S], BF16) for i in range(2)]
    KT = [sb(f"KT{i}", [128, S], BF16) for i in range(2)]
    QTa = [Buf(f"QTa{i}", QT[i].ap, parent=QT[i]) for i in range(2)]
    KTa = [Buf(f"KTa{i}", KT[i].ap, parent=KT[i]) for i in range(2)]
    V_all = sb("V_all", [128, NT, 8, 66], BF16)
    PT = [sb(f"PT{i}", [128, 512], BF16) for i in range(4)]
    sq = [Buf(f"sq{i}", hbf[i].ap[:, 0:512], parent=hbf[i]) for i in range(2)]
    T1h = Buf("T1h", T1.ap, parent=T1)
    T2h = Buf("T2h", T2.ap, parent=T2)
    TMPh = Buf("TMPh", TMP.ap, parent=TMP)
    endB = cur[0]

    cur[0] = arenaBC
    arena[0] = "C"
    w_o = sb("w_o", [128, 8, 1024], BF16)
    g_ffn = sb("g_ffn", [128, 1024], F32)
    OT0 = sb("OT0", [128, 8, 512], BF16)
    xr = [sb(f"xr{i}", [128, 1024], F32) for i in range(2)]
    x1_0 = [sb(f"x1_0_{i}", [128, 1024], F32) for i in range(4)]
    hbf2 = [sb(f"hbf2_{i}", [128, 1024], BF16) for i in range(2)]
    h2T0 = sb("h2T0", [128, 8, 512], BF16)
    g_foxC = sb("g_foxC", [128, 512], F32)
    g_mlaC = sb("g_mlaC", [128, 512], F32)
    NWGU, NWD = 4, 6
    wgu = [sb(f"wgu{i}", [128, 2, 8, 128], BF16) for i in range(NWGU)]
    g_fin = sb("g_fin", [128, 1024], F32)
    OT = [OT0, sb("OT1", [128, 8, 512], BF16)]
    h2T = [h2T0, sb("h2T1", [128, 8, 512], BF16)]
    x1 = [x1_0, [sb(f"x1_1_{i}", [128, 1024], F32) for i in range(4)]]
    aT = [sb(f"aT{i}", [128, 512], BF16) for i in range(NFC)]
    wd = [sb(f"wd{i}", [128, 2, 512], BF16) for i in range(NWD)]
    sg = [sb(f"sg{i}", [128, 512], BF16) for i in range(2)]
    outt = [sb(f"outt{i}", [128, 1024], F32) for i in range(2)]
    endC = cur[0]

    PS = [Buf(f"ps{i}", nc.alloc_psum_tensor(f"ps{i}", [128, 512], F32).ap(), excl=True) for i in range(8)]

    for b1 in Buf.registry:
        if b1.arena == "B":
            for b2 in Buf.registry:
                if b2.arena == "C" and b1.rng[0] < b2.rng[1] and b2.rng[0] < b1.rng[1]:
                    b1.alias.append(b2)
                    b2.alias.append(b1)

    win_bB = Buf("win_bB", win_b)
    win_qkB = Buf("win_qkB", win_b)
    wo_bB = Buf("wo_bB", wo_b)
    wgu_bB = Buf("wgu_bB", wgu_b)
    wd_bB = Buf("wd_bB", wd_b)

    stat_i = [0]

    def new_stat():
        stat_i[0] += 1
        return stats[stat_i[0] % len(stats)]

    cx.op("pool", lambda g: g.memset(ident.ap, 0.0), writes=[ident])
    cx.op("pool", lambda g: g.affine_select(out=ident.ap, in_=ident.ap, pattern=[[-1, 128]],
                                            compare_op=ALU.not_equal, fill=1.0, base=0,
                                            channel_multiplier=1), reads=[ident], writes=[ident])
    cx.op("pool", lambda g: g.memset(maskT.ap, 0.0), writes=[maskT])
    cx.op("pool", lambda g: g.affine_select(out=maskT.ap, in_=maskT.ap, pattern=[[1, 128]],
                                            compare_op=ALU.is_ge, fill=-30000.0, base=0,
                                            channel_multiplier=-1), reads=[maskT], writes=[maskT])
    cx.op("pool", lambda g: g.memset(ones_bf.ap, 1.0), writes=[ones_bf])
    cx.op("pool", lambda g: g.memset(ones8.ap, 1.0), writes=[ones8])
    cx.dma("sp", freq.ap, freq_d, writes=[freq])
    cx.dma("sp", negb.ap[0:8, :], bf_d, writes=[negb])
    for c2 in range(2):
        cx.dma("sp", qg.ap[:, c2:c2 + 1], qg_d[c2 * 128:(c2 + 1) * 128, :], writes=[qg])
    cx.dma("sp", kvg.ap, kvg_d, writes=[kvg])
    cx.op("dve", lambda v: v.tensor_scalar(out=negb.ap[0:8, :], in0=negb.ap[0:8, :], scalar1=-1.0,
                                           scalar2=None, op0=ALU.mult), reads=[negb], writes=[negb])
    cx.dma("pool", win_b[:, 1024:1960], w_in_d[:, 1024:1960], writes=[win_bB])
    cx.dma("pool", win_b[:, 1960:1976], w_in_d[:, C_KR + 16:C_KR + 32], writes=[win_bB])
    cx.dma("pool", win_b[:, 1976:1992], w_in_d[:, C_KR:C_KR + 16], writes=[win_bB])
    win_v0 = win_b.rearrange("(c p) e -> p c e", p=128)
    w0sem = {k: Buf("w0sem_" + k, w_in.ap) for k in ("w_vf", "w_kr", "w_lat", "w_qk")}
    cx.dma("pool", w_in.ap[:, :, 1024:1544], win_v0[:, :, 1024:1544], reads=[win_bB], writes=[w_vf], semof=w0sem["w_vf"])
    cx.dma("pool", w_in.ap[:, :, 1864:WIN], win_v0[:, :, 1864:WIN], reads=[win_bB], writes=[w_kr], semof=w0sem["w_kr"])
    cx.dma("pool", w_in.ap[:, :, 1544:1864], win_v0[:, :, 1544:1864], reads=[win_bB], writes=[w_lat], semof=w0sem["w_lat"])
    for rb in range(8):
        rows = slice(rb * 128, (rb + 1) * 128)
        for j in range(2):
            cx.dma("pool", win_b[rows, 0:1024].rearrange("d (h j e) -> d h j e", h=8, j=2)[:, :, j, :],
                   w_in_d[rows, j * 512:(j + 1) * 512].rearrange("d (h e) -> d h e", h=8), writes=[win_qkB])
    cx.dma("pool", w_in.ap[:, :, 0:1024], win_v0[:, :, 0:1024], reads=[win_qkB], writes=[w_qk], semof=w0sem["w_qk"])
    cx.dma("pool", wuq.ap, w_uq_d.rearrange("(c p) e -> p c e", p=128), writes=[wuq])
    cx.op("pool", lambda g: g.memset(wuqs.ap, 0.0), writes=[wuqs])
    wuq_v = w_uq_d.rearrange("(c p) (h e) -> p c h e", p=128, h=8)
    for c2 in range(2):
        cx.dma("pool", wuqs.ap[:, c2, :, 64:80], wuq_v[:, c2, :, 80:96], writes=[wuqs])
        cx.dma("pool", wuqs.ap[:, c2, :, 80:96], wuq_v[:, c2, :, 64:80], writes=[wuqs])
    cx.dma("pool", wukv.ap, w_ukv_d, writes=[wukv])
    deferred = []
    deferred.append(lambda: cx.dma("pool", wo_b, w_o_d, writes=[wo_bB]))
    wg_v = w_g_d.rearrange("(c p) (fc f) -> fc p c f", p=128, f=128)
    wu_v = w_u_d.rearrange("(c p) (fc f) -> fc p c f", p=128, f=128)
    for fc in range(NFC):
        deferred.append(lambda fc=fc: cx.dma("pool", wgu_b[fc, :, 0, :, :], wg_v[fc], writes=[wgu_bB]))
        deferred.append(lambda fc=fc: cx.dma("pool", wgu_b[fc, :, 1, :, :], wu_v[fc], writes=[wgu_bB]))
    for r in range(0, DFF, 704):
        deferred.append(lambda r=r: cx.dma("pool", wd_b[r:r + 704, :], w_d_d[r:r + 704, :], writes=[wd_bB]))

    def run_deferred(k):
        for _ in range(k):
            if deferred:
                deferred.pop(0)()

    def rstd_from(st, n):
        cx.op("act", lambda a: a.activation(out=st.ap[:, 1:2], in_=st.ap[:, 0:1], func=AF.Ln,
                                            scale=1.0 / n, bias=EPS), reads=[st], writes=[st])
        cx.op("act", lambda a: a.activation(out=st.ap[:, 1:2], in_=st.ap[:, 1:2], func=AF.Exp,
                                            scale=-0.5), reads=[st], writes=[st])

    def rms_to_bf(xb, xap, gb, hb):
        st = new_stat()
        cx.op("act", lambda a: a.activation(out=hb.ap, in_=xap, func=AF.Square,
                                            accum_out=st.ap[:, 0:1]), reads=[xb], writes=[hb, st])
        rstd_from(st, 1024)
        cx.op("dve", lambda v: v.scalar_tensor_tensor(out=hb.ap, in0=xap, scalar=st.ap[:, 1:2],
                                                      in1=gb.ap, op0=ALU.mult, op1=ALU.mult),
              reads=[xb, st, gb], writes=[hb])

    def transpose_pe(src_b, src_ap, bank):
        psb = bank.ap.bitcast(BF16)
        with cx.grp("pe", reads=[src_b, ident], writes=[bank]) as h:
            for c in range(8):
                h[0] = T.transpose(out=psb[:, c * 128:(c + 1) * 128], in_=src_ap[:, c * 128:(c + 1) * 128],
                                   identity=ident.ap)

    def transpose_evac(bank, dst_b, dst_ap):
        psb = bank.ap.bitcast(BF16)
        cx.op("dve", lambda v: v.tensor_copy(out=dst_ap, in_=psb.rearrange("p (c t) -> p c t", c=8)),
              reads=[bank], writes=[dst_b])

    def transpose_to(src_b, src_ap, bank, dst_b, dst_ap, evac="dve"):
        psb = bank.ap.bitcast(BF16)
        with cx.grp("pe", reads=[src_b, ident], writes=[bank]) as h:
            for c in range(8):
                h[0] = T.transpose(out=psb[:, c * 128:(c + 1) * 128], in_=src_ap[:, c * 128:(c + 1) * 128],
                                   identity=ident.ap)
        if evac == "act":
            cx.op("act", lambda a: a.copy(out=dst_ap, in_=psb.rearrange("p (c t) -> p c t", c=8)),
                  reads=[bank], writes=[dst_b])
        else:
            cx.op("dve", lambda v: v.tensor_copy(out=dst_ap, in_=psb.rearrange("p (c t) -> p c t", c=8)),
                  reads=[bank], writes=[dst_b])

    def attention(QTb, QTab, KTb, KTab, dk, hv, scale, ocol, filler, filler_start=0, filler_rate=2):
        qt, kt_ = QTb.ap, KTb.ap
        steps = []
        for j in range(NCH):
            for kt in range(4 * j + 4):
                steps.append((j, kt))
        nst = len(steps)
        sbank = [PS[0], PS[1], PS[2]]
        obank = [PS[3], PS[4]]

        def qk(n):
            j, kt = steps[n]
            i = kt - 4 * j
            q0 = max(0, i) * 128
            bk = sbank[n % 3]
            with cx.grp("pe", reads=[QTb, QTab, KTb, KTab, ident, maskT], writes=[bk]) as h:
                h[0] = T.matmul(bk.ap[:, q0:512], lhsT=kt_[0:dk, kt * 128:(kt + 1) * 128],
                                rhs=qt[0:dk, j * 512 + q0:(j + 1) * 512], start=True, stop=(i < 0))
                if i >= 0:
                    h[0] = T.matmul(bk.ap[:, q0:q0 + 128], lhsT=ident.ap, rhs=maskT.ap, start=False, stop=True)

        def ex(n):
            j, kt = steps[n]
            q0 = max(0, kt - 4 * j) * 128
            bk = sbank[n % 3]
            pt = PT[n % 4]
            cx.op("act", lambda a: a.activation(out=pt.ap[:, q0:512], in_=bk.ap[:, q0:512], func=AF.Exp,
                                                scale=scale), reads=[bk], writes=[pt])

        def pv(n):
            j, kt = steps[n]
            i = kt - 4 * j
            pt = PT[n % 4]
            ob = obank[j % 2]
            ov = ob.ap[:, 0:260].rearrange("p (a b) -> p a b", a=4)
            with cx.grp("pe", reads=[pt, V_all], writes=[ob]) as h:
                for qb in range(max(0, i), 4):
                    h[0] = T.matmul(ov[:, qb, :], lhsT=pt.ap[:, qb * 128:(qb + 1) * 128],
                                    rhs=V_all.ap[:, kt, hv, 0:65], start=(kt == 0 and qb == 0),
                                    stop=(kt == 4 * j + qb), skip_group_check=True)
            if kt == 4 * j + 3:
                rr = rrs[j % 2]
                cx.op("dve", lambda v: v.reciprocal(out=rr.ap, in_=ov[:, :, 64]), reads=[ob], writes=[rr])
                cx.op("dve", lambda v: v.tensor_tensor(
                    out=O_all.ap[:, 4 * j:4 * j + 4, ocol:ocol + 64], in0=ov[:, :, 0:64],
                    in1=rr.ap.unsqueeze(2).broadcast_to([128, 4, 64]), op=ALU.mult),
                    reads=[ob, rr], writes=[O_all])

        qk(0)
        qk(1)
        for n in range(nst):
            ex(n)
            if n + 2 < nst:
                qk(n + 2)
            pv(n)
            if filler is not None and n >= filler_start:
                for _ in range(filler_rate):
                    try:
                        next(filler)
                    except StopIteration:
                        filler = None
                        break
        if filler is not None:
            for _ in filler:
                pass

    pbank = [PS[5], PS[6]]
    fbank = [PS[5], PS[6], PS[7]]
    pbi = [0]

    def next_pbank():
        pbi[0] += 1
        return pbank[pbi[0] % 2]

    def next_fbank():
        pbi[0] += 1
        return fbank[pbi[0] % 3]

    def fox_proj(h, b):
        cst = T2.ap.bitcast(BF16).rearrange("p (a s) -> p a s", a=2)
        cx.op("pool", lambda g: g.memset(QT[b].ap[64:68, :], -1.0), writes=[QTa[b]])
        cx.op("pool", lambda g: g.memset(KT[b].ap[64:68, :], 1.0), writes=[KTa[b]])
        for r_ in range(2):
            cx.dma("pool", QT[b].ap[64 + r_:65 + r_, :], cst[h:h + 1, r_, :], reads=[T2], writes=[QTa[b]])
            cx.dma("pool", KT[b].ap[66 + r_:67 + r_, :], cst[h:h + 1, r_, :], reads=[T2], writes=[KTa[b]])
        yield
        for n in range(NCH):
            bk = next_fbank()
            with cx.grp("pe", reads=[w_qk] + hTc, writes=[bk]) as hh:
                for c in range(8):
                    hh[0] = T.matmul(bk.ap, lhsT=w_in.ap[:, c, h * 128:(h + 1) * 128],
                                     rhs=hT.ap[:, c, n * 512:(n + 1) * 512], start=(c == 0), stop=(c == 7))
                    if c == 3:
                        yield
            cx.op("dve", lambda v: v.tensor_copy(out=QT[b].ap[0:64, n * 512:(n + 1) * 512], in_=bk.ap[0:64, :]),
                  reads=[bk], writes=[QT[b]])
            cx.op("dve", lambda v: v.tensor_copy(out=KT[b].ap[0:64, n * 512:(n + 1) * 512], in_=bk.ap[64:128, :]),
                  reads=[bk], writes=[KT[b]])
            yield

    rope_i = [0]

    def mla_proj(h, b):
        cx.op("pool", lambda g: g.tensor_copy(out=KT[b].ap[64:96, :], in_=k_peT.ap[64:96, :]),
              reads=[k_peT], writes=[KTa[b]])
        yield
        if 3 <= h <= 6:
            group_norm_tile(0, h - 3, g_foxC)
            yield
        for n in range(NCH):
            cs = slice(n * 512, (n + 1) * 512)
            rb = ropeS
            rA, rB = rb.ap[64:96, 0:512], rb.ap[64:96, 512:1024]
            bq = next_fbank()
            with cx.grp("pe", reads=[wuq, q_latT], writes=[bq]) as hh:
                for c2 in range(2):
                    hh[0] = T.matmul(bq.ap[0:96, :], lhsT=wuq.ap[:, c2, h * 96:(h + 1) * 96],
                                     rhs=q_latT.ap[:, c2, cs], start=(c2 == 0), stop=(c2 == 1))
            cx.op("dve", lambda v: v.tensor_tensor(out=QT[b].ap[0:64, cs], in0=bq.ap[0:64, :],
                                                   in1=TMP.ap[0:64, cs], op=ALU.mult),
                  reads=[bq, TMP], writes=[QT[b]])
            cx.op("dve", lambda v: v.tensor_tensor(out=rA, in0=bq.ap[64:96, :],
                                                   in1=T1.ap[64:96, cs], op=ALU.mult),
                  reads=[bq, T1h], writes=[rb])
            yield
            bs = next_fbank()
            with cx.grp("pe", reads=[wuqs, q_latT], writes=[bs]) as hh:
                for c2 in range(2):
                    hh[0] = T.matmul(bs.ap[0:96, :], lhsT=wuqs.ap[:, c2, h, :],
                                     rhs=q_latT.ap[:, c2, cs], start=(c2 == 0), stop=(c2 == 1))
            cx.op("dve", lambda v: v.tensor_tensor(out=rB, in0=bs.ap[64:96, :],
                                                   in1=T2.ap[64:96, cs], op=ALU.mult),
                  reads=[bs, T2h, rb], writes=[rb])
            cx.op("pool", lambda g: g.tensor_tensor(out=QT[b].ap[64:96, cs], in0=rA, in1=rB, op=ALU.add),
                  reads=[rb], writes=[QTa[b]])
            yield
            bk = next_fbank()
            with cx.grp("pe", reads=[wukv, kv_latT], writes=[bk]) as hh:
                hh[0] = T.matmul(bk.ap[0:64, :], lhsT=wukv.ap[:, h * 128:h * 128 + 64],
                                 rhs=kv_latT.ap[:, cs], start=True, stop=True)
            cx.op("dve", lambda v: v.tensor_tensor(out=KT[b].ap[0:64, cs], in0=bk.ap[0:64, :],
                                                   in1=T2.ap[0:64, cs], op=ALU.mult),
                  reads=[bk, T2], writes=[KT[b]])
            yield

    def drain(gen):
        for _ in gen:
            pass

    def group_norm_tile(gi, tt, gb):
        oap = O_all.ap[:, tt, gi * 512:(gi + 1) * 512]
        st = new_stat()
        jb = sqj[tt % 2]
        cx.op("act", lambda a: a.activation(out=jb.ap, in_=oap, func=AF.Square, accum_out=st.ap[:, 0:1]),
              reads=[O_all], writes=[jb, st])
        rstd_from(st, 512)
        cx.op("dve", lambda v: v.scalar_tensor_tensor(out=oap, in0=oap, scalar=st.ap[:, 1:2], in1=gb.ap,
                                                      op0=ALU.mult, op1=ALU.mult),
              reads=[O_all, st, gb], writes=[O_all])

    def group_norm(gi, gb):
        for tt in range(NT):
            group_norm_tile(gi, tt, gb)

    win_v = win_b.rearrange("(c p) e -> p c e", p=128)

    def load_w(buf, c0, c1):
        src = win_qkB if c0 == 0 else win_bB
        cx.dma("sp", w_in.ap[:, :, c0:c1], win_v[:, :, c0:c1], reads=[src], writes=[buf])

    def norm1_gen(sq_, tbanks):
        cx.dma("sp", g_mix.ap, g_mix_d.broadcast_to([128, D]), writes=[g_mix])

        def xload(t):
            cx.dma("sp", xin[t % 3].ap, x[sq_, t * 128:(t + 1) * 128, :], writes=[xin[t % 3]])

        xload(0)
        xload(1)
        yield
        for tt in range(NT + 2):
            if tt + 2 < NT:
                xload(tt + 2)
            if tt < NT:
                rms_to_bf(xin[tt % 3], xin[tt % 3].ap, g_mix, hbf[tt % 2])
            if 1 <= tt <= NT:
                t = tt - 1
                transpose_pe(hbf[t % 2], hbf[t % 2].ap, tbanks[t % 2])
            if 2 <= tt <= NT + 1:
                t = tt - 2
                transpose_evac(tbanks[t % 2], hTc[t // 4], hT.ap[:, :, t * 128:(t + 1) * 128])
            if sq_ > 0:
                if tt == 0:
                    load_w(w_vf, 1024, 1544)
                if tt == 2:
                    load_w(w_kr, 1864, WIN)
                if tt == 6:
                    load_w(w_qk, 0, 1024)
                if tt == 10:
                    load_w(w_lat, 1544, 1864)
            yield

    wgu_i = [0]
    wd_i = [0]
    for s in range(nseq):
        T1i = T1.ap.bitcast(I32)

        def fox_f_chunk(n):
            bk = next_pbank()
            with cx.grp("pe", reads=[w_vf, hTc[n]], writes=[bk]) as hh:
                for c in range(8):
                    hh[0] = T.matmul(bk.ap[0:8, :], lhsT=w_in.ap[:, c, C_F:C_F + 8],
                                     rhs=hT.ap[:, c, n * 512:(n + 1) * 512], start=(c == 0), stop=(c == 7))
            cx.op("act", lambda a: a.activation(out=TMP.ap[0:8, n * 512:(n + 1) * 512], in_=bk.ap[0:8, :],
                                                func=AF.Exp, scale=-1.0, bias=negb.ap[0:8, 0:1]),
                  reads=[bk, negb], writes=[TMP])

        def fox_v_tile(tt):
            bk = next_pbank()
            with cx.grp("pe", reads=[w_vf, hTc[tt // 4]], writes=[bk]) as hh:
                for c in range(8):
                    hh[0] = T.matmul(bk.ap, lhsT=hT.ap[:, c, tt * 128:(tt + 1) * 128],
                                     rhs=w_in.ap[:, c, 1024:1536], start=(c == 0), stop=(c == 7))
            cx.op("dve", lambda v: v.tensor_copy(out=V_all.ap[:, tt, :, 0:64],
                                                 in_=bk.ap.rearrange("p (h e) -> p h e", h=8)),
                  reads=[bk], writes=[V_all])

        def build_rope_tables():
            cx.op("dve", lambda v: v.tensor_copy(out=TMP.ap[64:96, :], in_=T1i[64:96, :]), reads=[T1h], writes=[TMPh])
            yield
            for (tab, fcol) in ((T1h, 0), (T2h, 2)):
                cx.op("dve", lambda v: v.tensor_scalar(out=tab.ap[64:96, :], in0=TMP.ap[64:96, :],
                                                       scalar1=freq.ap[64:96, fcol:fcol + 1],
                                                       scalar2=freq.ap[64:96, fcol + 1:fcol + 2],
                                                       op0=ALU.mult, op1=ALU.add),
                      reads=[TMPh, freq], writes=[tab])
                yield
                for half in range(2):
                    hs = slice(half * 1024, (half + 1) * 1024)
                    ni = ropeS.ap.bitcast(I32)
                    cx.op("dve", lambda v: v.tensor_scalar(out=ni[64:96, :], in0=tab.ap[64:96, hs],
                                                           scalar1=1.0 / TWO_PI, scalar2=None, op0=ALU.mult),
                          reads=[tab], writes=[ropeS])
                    yield
                    for cw in (CW1, CW2):
                        cx.op("dve", lambda v: v.scalar_tensor_tensor(out=tab.ap[64:96, hs], in0=ni[64:96, :],
                                                                      scalar=-cw, in1=tab.ap[64:96, hs],
                                                                      op0=ALU.mult, op1=ALU.add),
                              reads=[ropeS, tab], writes=[tab])
                        yield
                cx.op("dve", lambda v: v.tensor_scalar(out=tab.ap[64:96, :], in0=tab.ap[64:96, :], scalar1=3.1415925,
                                                       scalar2=-3.1415925, op0=ALU.min, op1=ALU.max),
                      reads=[tab], writes=[tab])
                yield
                cx.op("act", lambda a: a.activation(out=tab.ap[64:96, :], in_=tab.ap[64:96, :], func=AF.Sin),
                      reads=[tab], writes=[tab])

        def kpe_chunk(n):
            cs = slice(n * 512, (n + 1) * 512)
            rA, rB = ropeS.ap[64:96, 0:512], ropeS.ap[64:96, 512:1024]
            ba = next_pbank()
            with cx.grp("pe", reads=[w_kr, hTc[n]], writes=[ba]) as hh:
                for c in range(8):
                    hh[0] = T.matmul(ba.ap[0:96, :], lhsT=w_in.ap[:, c, C_KR - 64:C_KR + 32],
                                     rhs=hT.ap[:, c, cs], start=(c == 0), stop=(c == 7))
            cx.op("dve", lambda v: v.tensor_tensor(out=rA, in0=ba.ap[64:96, :], in1=T1.ap[64:96, cs], op=ALU.mult),
                  reads=[ba, T1h], writes=[ropeS])
            bb = next_pbank()
            with cx.grp("pe", reads=[w_kr, hTc[n]], writes=[bb]) as hh:
                for c in range(8):
                    hh[0] = T.matmul(bb.ap[0:96, :], lhsT=w_in.ap[:, c, 1960 - 64:1992],
                                     rhs=hT.ap[:, c, cs], start=(c == 0), stop=(c == 7))
            cx.op("dve", lambda v: v.tensor_tensor(out=rB, in0=bb.ap[64:96, :], in1=T2.ap[64:96, cs], op=ALU.mult),
                  reads=[bb, T2h, ropeS], writes=[ropeS])
            cx.op("pool", lambda g: g.tensor_tensor(out=k_peT.ap[64:96, cs], in0=rA, in1=rB, op=ALU.add),
                  reads=[ropeS], writes=[k_peT])

        if s == 0:
            drain(norm1_gen(0, (PS[3], PS[4])))
        T1i = T1.ap.bitcast(I32)
        cx.dma("sp", T1i[64:96, :], pos[s:s + 1, :].broadcast_to([32, S]), writes=[T1h])
        rt = build_rope_tables()
        for n in range(NCH):
            fox_f_chunk(n)
            for t4 in range(4):
                fox_v_tile(4 * n + t4)
                for _ in range(2):
                    next(rt, None)
        drain(rt)
        for n in range(NCH):
            kpe_chunk(n)
        cx.op("pool", lambda g: g.memset(V_all.ap[:, :, :, 64:65], 1.0), writes=[V_all])
        cx.op("act", lambda a: a.activation(out=TMP.ap[0:8, :], in_=TMP.ap[0:8, :], func=AF.Ln, bias=1.0),
              reads=[TMP], writes=[TMP])
        cx.op("dve", lambda v: v.tensor_tensor_scan(out=T1.ap[0:8, :], data0=ones8.ap[0:8, 0:1].broadcast_to([8, S]),
                                                    data1=TMP.ap[0:8, :], initial=0.0, op0=ALU.mult, op1=ALU.add),
              reads=[TMP, ones8], writes=[T1])
        cst = T2.ap.bitcast(BF16).rearrange("p (a s) -> p a s", a=2)
        cx.op("dve", lambda v: v.tensor_scalar(out=cst[0:8, 0, :], in0=T1.ap[0:8, :], scalar1=-8.0, scalar2=None,
                                               op0=ALU.mult), reads=[T1], writes=[T2])
        cx.op("dve", lambda v: v.scalar_tensor_tensor(out=cst[0:8, 1, :], in0=T1.ap[0:8, :], scalar=-8.0,
                                                      in1=cst[0:8, 0, :], op0=ALU.mult, op1=ALU.subtract),
              reads=[T1, T2], writes=[T2])

        drain(fox_proj(0, 0))
        for h in range(8):
            filler = fox_proj(h + 1, (h + 1) % 2) if h < 7 else None
            attention(QT[h % 2], QTa[h % 2], KT[h % 2], KTa[h % 2], 68, h, 0.125, h * 64, filler)
            run_deferred(5)

        for n in range(NCH):
            cs = slice(n * 512, (n + 1) * 512)
            ssq = PS[7]
            for c2 in range(2):
                bk = next_pbank()
                with cx.grp("pe", reads=[w_lat] + hTc, writes=[bk]) as hh:
                    for c in range(8):
                        hh[0] = T.matmul(bk.ap, lhsT=w_in.ap[:, c, C_QL + c2 * 128:C_QL + (c2 + 1) * 128],
                                         rhs=hT.ap[:, c, cs], start=(c == 0), stop=(c == 7))
                sqb = sq[c2]
                cx.op("act", lambda a: a.activation(out=sqb.ap, in_=bk.ap, func=AF.Square), reads=[bk], writes=[sqb])
                cx.op("dve", lambda v: v.tensor_scalar(out=q_latT.ap[:, c2, cs], in0=bk.ap, scalar1=qg.ap[:, c2:c2 + 1],
                                                       scalar2=None, op0=ALU.mult), reads=[bk, qg], writes=[q_latT])
                with cx.grp("pe", reads=[sqb, ones_bf], writes=[ssq]) as hh:
                    hh[0] = T.matmul(ssq.ap, lhsT=ones_bf.ap, rhs=sqb.ap, start=(c2 == 0), stop=(c2 == 1))
            cx.op("act", lambda a: a.activation(out=TMP.ap[0:96, cs], in_=ssq.ap[0:96, :], func=AF.Ln,
                                                scale=1.0 / 256, bias=EPS), reads=[ssq], writes=[TMP, TMPh])
            cx.op("act", lambda a: a.activation(out=TMP.ap[0:96, cs], in_=TMP.ap[0:96, cs], func=AF.Exp, scale=-0.5),
                  reads=[TMP, TMPh], writes=[TMP, TMPh])
        for tab in (T1h, T2h):
            cx.op("dve", lambda v: v.tensor_tensor(out=tab.ap[64:96, :], in0=tab.ap[64:96, :], in1=TMP.ap[64:96, :],
                                                   op=ALU.mult), reads=[tab, TMPh, k_peT], writes=[tab])
        for n in range(NCH):
            cs = slice(n * 512, (n + 1) * 512)
            ssq = PS[7]
            bk = next_pbank()
            with cx.grp("pe", reads=[w_lat, w_kr] + hTc, writes=[bk]) as hh:
                for c in range(8):
                    hh[0] = T.matmul(bk.ap, lhsT=w_in.ap[:, c, C_KV:C_KV + 128], rhs=hT.ap[:, c, cs],
                                     start=(c == 0), stop=(c == 7))
            sqb = sq[n % 2]
            cx.op("act", lambda a: a.activation(out=sqb.ap, in_=bk.ap, func=AF.Square), reads=[bk], writes=[sqb])
            cx.op("dve", lambda v: v.tensor_scalar(out=kv_latT.ap[:, cs], in0=bk.ap, scalar1=kvg.ap[:, 0:1],
                                                   scalar2=None, op0=ALU.mult), reads=[bk, kvg], writes=[kv_latT])
            with cx.grp("pe", reads=[sqb, ones_bf], writes=[ssq]) as hh:
                hh[0] = T.matmul(ssq.ap, lhsT=ones_bf.ap, rhs=sqb.ap, start=True, stop=True)
            cx.op("act", lambda a: a.activation(out=T2.ap[0:64, cs], in_=ssq.ap[0:64, :], func=AF.Ln,
                                                scale=1.0 / 128, bias=EPS), reads=[ssq], writes=[T2])
            cx.op("act", lambda a: a.activation(out=T2.ap[0:64, cs], in_=T2.ap[0:64, cs], func=AF.Exp, scale=-0.5),
                  reads=[T2], writes=[T2])
            tb = next_pbank()
            with cx.grp("pe", reads=[sqb, ones_bf], writes=[tb]) as hh:
                for ttl in range(4):
                    hh[0] = T.matmul(tb.ap[:, ttl:ttl + 1], lhsT=sqb.ap[:, ttl * 128:(ttl + 1) * 128],
                                     rhs=ones_bf.ap[:, 0:1], start=True, stop=True, skip_group_check=True)
            cx.op("act", lambda a: a.activation(out=rkv_tok.ap[:, 4 * n:4 * n + 4], in_=tb.ap[:, 0:4], func=AF.Ln,
                                                scale=1.0 / 128, bias=EPS), reads=[tb], writes=[rkv_tok])
        cx.op("act", lambda a: a.activation(out=rkv_tok.ap, in_=rkv_tok.ap, func=AF.Exp, scale=-0.5),
              reads=[rkv_tok], writes=[rkv_tok])
        wv = wukv.ap.rearrange("p (h e) -> p h e", h=8)[:, :, 64:128]
        for tt in range(NT):
            bk = next_pbank()
            with cx.grp("pe", reads=[wukv, kv_latT], writes=[bk]) as hh:
                hh[0] = T.matmul(bk.ap.rearrange("p (h e) -> p h e", h=8), lhsT=kv_latT.ap[:, tt * 128:(tt + 1) * 128],
                                 rhs=wv, start=True, stop=True)
            cx.op("dve", lambda v: v.tensor_scalar(out=V_all.ap[:, tt, :, 0:64],
                                                   in0=bk.ap.rearrange("p (h e) -> p h e", h=8),
                                                   scalar1=rkv_tok.ap[:, tt:tt + 1], scalar2=None, op0=ALU.mult),
                  reads=[bk, rkv_tok], writes=[V_all])
        def load_wgu(fc):
            slot = wgu[wgu_i[0] % NWGU]
            wgu_i[0] += 1
            cx.dma("sp", slot.ap, wgu_b[fc], reads=[wgu_bB], writes=[slot])
            return slot

        def load_wd(dh, f2):
            slot = wd[wd_i[0] % NWD]
            wd_i[0] += 1
            cx.dma("sp", slot.ap, wd_b[f2 * 256:(f2 + 1) * 256, dh * 512:(dh + 1) * 512].rearrange(
                "(a p) e -> p a e", p=128), reads=[wd_bB], writes=[slot])
            return slot

        def stageA(n, p, obk=(PS[0], PS[1]), tbk=(PS[2], PS[3]), do_gn=False):
            ot, h2, xx = OT[p], h2T[p], x1[p]

            def gn(ttl):
                group_norm_tile(1, 4 * n + ttl, g_mlaC)

            def trO(ttl):
                transpose_to(O_all, O_all.ap[:, 4 * n + ttl, :], tbk[ttl % 2], ot, ot.ap[:, :, ttl * 128:(ttl + 1) * 128])

            def xload(ttl):
                tt = 4 * n + ttl
                cx.dma("sp", xr[ttl % 2].ap, x[s, tt * 128:(tt + 1) * 128, :], writes=[xr[ttl % 2]])

            def oproj(ttl, dh):
                bk = obk[dh]
                xb = xr[ttl % 2]
                with cx.grp("pe", reads=[ot, w_o], writes=[bk]) as hh:
                    for c in range(8):
                        hh[0] = T.matmul(bk.ap, lhsT=ot.ap[:, c, ttl * 128:(ttl + 1) * 128],
                                         rhs=w_o.ap[:, c, dh * 512:(dh + 1) * 512], start=(c == 0), stop=(c == 7))
                cx.op("dve", lambda v: v.tensor_tensor(out=xx[ttl].ap[:, dh * 512:(dh + 1) * 512], in0=bk.ap,
                                                       in1=xb.ap[:, dh * 512:(dh + 1) * 512], op=ALU.add),
                      reads=[bk, xb], writes=[xx[ttl]])

            def nrm(ttl):
                rms_to_bf(xx[ttl], xx[ttl].ap, g_ffn, hbf2[ttl % 2])

            def trH(ttl):
                hb = hbf2[ttl % 2]
                transpose_to(hb, hb.ap, tbk[ttl % 2], h2, h2.ap[:, :, ttl * 128:(ttl + 1) * 128])

            pre = [[(gn, 0), (xload, 0)], [(gn, 1), (xload, 1)], [(gn, 2)], [(gn, 3)], [(trO, 0)]] if do_gn else \
                [[(xload, 0)], [(trO, 0), (xload, 1)]]
            steps = pre + [
                [(trO, 1)], [(trO, 2)], [(trO, 3)],
                [(oproj, 0, 0)], [(oproj, 0, 1), (nrm, 0)],
                [(oproj, 1, 0), (xload, 2)], [(oproj, 1, 1), (nrm, 1), (trH, 0)],
                [(oproj, 2, 0), (xload, 3)], [(oproj, 2, 1), (nrm, 2), (trH, 1)],
                [(oproj, 3, 0)], [(oproj, 3, 1), (nrm, 3), (trH, 2)], [], [(trH, 3)],
            ]
            for st_ in steps:
                for f_ in st_:
                    f_[0](*f_[1:])
                yield

        cx.dma("sp", w_o.ap, wo_b.rearrange("(c p) e -> p c e", p=128), reads=[wo_bB], writes=[w_o])
        cx.dma("sp", g_ffn.ap, g_ffn_d.broadcast_to([128, D]), writes=[g_ffn])
        cx.dma("sp", g_mlaC.ap, g_mla_d.broadcast_to([128, 512]), writes=[g_mlaC])
        cx.dma("sp", g_foxC.ap, g_fox_d.broadcast_to([128, 512]), writes=[g_foxC])
        drain(mla_proj(0, 0))
        mscale = 96.0 ** -0.5
        for h in range(8):
            if h < 7:
                attention(QT[h % 2], QTa[h % 2], KT[h % 2], KTa[h % 2], 96, h, mscale, 512 + h * 64,
                          mla_proj(h + 1, (h + 1) % 2))
            else:
                attention(QT[h % 2], QTa[h % 2], KT[h % 2], KTa[h % 2], 96, h, mscale, 512 + h * 64,
                          stageA(0, 0, obk=(PS[5], PS[6]), tbk=(PS[7], PS[7]), do_gn=True), filler_start=4, filler_rate=1)
            run_deferred(2)

        run_deferred(100)
        cx.dma("sp", g_fin.ap, g_fin_d.broadcast_to([128, D]), writes=[g_fin])

        next_wq = None
        pending_final = []
        for n in range(NCH):
            p = n % 2
            h2, xx = h2T[p], x1[p]
            wq_slots = next_wq if next_wq else [load_wgu(i_) for i_ in range(NWGU - 1)]
            next_wq = None
            wd_reqs = [(dh_, f2_) for dh_ in range(2) for f2_ in range(NFC // 2)]
            wd_loaded = []

            def issue_wd():
                if len(wd_loaded) < len(wd_reqs):
                    dh_, f2_ = wd_reqs[len(wd_loaded)]
                    wd_loaded.append(load_wd(dh_, f2_))
            for fc in range(NFC):
                slot = wq_slots.pop(0)
                if fc + NWGU - 1 < NFC:
                    wq_slots.append(load_wgu(fc + NWGU - 1))
                if 2 <= fc < 2 + NWD:
                    issue_wd()
                if fc in (2, 4) and pending_final:
                    pending_final.pop(0)()
                    pending_final.pop(0)()
                if fc == 10 and n + 1 == NCH and s + 1 < nseq:
                    nxt_norm = norm1_gen(s + 1, (PS[0], PS[1]))
                    next(nxt_norm)
                if fc in (6, 12) and n + 1 < NCH:
                    for ttl_ in ((0, 1) if fc == 6 else (2, 3)):
                        group_norm_tile(0, 4 * (n + 1) + ttl_, g_foxC)
                        group_norm_tile(1, 4 * (n + 1) + ttl_, g_mlaC)
                bg = PS[(fc % 2) * 2]
                bu = PS[(fc % 2) * 2 + 1]
                for (bk, gi) in ((bg, 0), (bu, 1)):
                    with cx.grp("pe", reads=[slot, h2], writes=[bk]) as hh:
                        for c in range(8):
                            hh[0] = T.matmul(bk.ap, lhsT=slot.ap[:, gi, c, :], rhs=h2.ap[:, c, :],
                                             start=(c == 0), stop=(c == 7))
                sgb = sg[fc % 2]
                cx.op("act", lambda a: a.activation(out=sgb.ap, in_=bg.ap, func=AF.Silu), reads=[bg], writes=[sgb])
                cx.op("dve", lambda v: v.tensor_tensor(out=aT[fc].ap, in0=sgb.ap, in1=bu.ap, op=ALU.mult),
                      reads=[sgb, bu], writes=[aT[fc]])
            if n + 1 < NCH:
                next_wq = [load_wgu(i_) for i_ in range(NWGU - 1)]
            if n + 1 < NCH:
                nxt = stageA(n + 1, 1 - p)
            elif s + 1 < nseq:
                nxt = nxt_norm
            else:
                nxt = None
            for dh in range(2):
                dbank = [PS[4], PS[5], PS[6], PS[7]]
                for f2 in range(NFC // 2):
                    slot = wd_loaded[dh * (NFC // 2) + f2]
                    if f2 in (0, NFC // 2 - 1):
                        for ttl in range(4):
                            with cx.grp("pe", reads=[slot, aT[2 * f2], aT[2 * f2 + 1]], writes=[dbank[ttl]]) as hh:
                                for a_ in range(2):
                                    fc = 2 * f2 + a_
                                    hh[0] = T.matmul(dbank[ttl].ap, lhsT=aT[fc].ap[:, ttl * 128:(ttl + 1) * 128],
                                                     rhs=slot.ap[:, a_, :], start=(fc == 0), stop=(fc == NFC - 1))
                    else:
                        with cx.grp("pe", reads=[slot, aT[2 * f2], aT[2 * f2 + 1]], writes=[]) as hh:
                            for a_ in range(2):
                                fc = 2 * f2 + a_
                                for ttl in range(4):
                                    hh[0] = T.matmul(dbank[ttl].ap, lhsT=aT[fc].ap[:, ttl * 128:(ttl + 1) * 128],
                                                     rhs=slot.ap[:, a_, :], start=(fc == 0), stop=(fc == NFC - 1))
                    issue_wd()
                    if f2 == NFC // 2 - 1:
                        for ttl in range(4):
                            cx.op("dve", lambda v: v.tensor_tensor(out=xx[ttl].ap[:, dh * 512:(dh + 1) * 512],
                                                                   in0=dbank[ttl].ap,
                                                                   in1=xx[ttl].ap[:, dh * 512:(dh + 1) * 512],
                                                                   op=ALU.add),
                                  reads=[dbank[ttl], xx[ttl]], writes=[xx[ttl]])
                    if nxt is not None:
                        try:
                            next(nxt)
                        except StopIteration:
                            nxt = None
            if nxt is not None:
                drain(nxt)
            def final_tile(ttl, n=n, xx=xx):
                tt = 4 * n + ttl
                ob = outt[ttl % 2]
                st = new_stat()
                cx.op("act", lambda a: a.activation(out=ob.ap, in_=xx[ttl].ap, func=AF.Square,
                                                    accum_out=st.ap[:, 0:1]), reads=[xx[ttl]], writes=[ob, st])
                rstd_from(st, 1024)
                cx.op("dve", lambda v: v.scalar_tensor_tensor(out=ob.ap, in0=xx[ttl].ap, scalar=st.ap[:, 1:2],
                                                              in1=g_fin.ap, op0=ALU.mult, op1=ALU.mult),
                      reads=[xx[ttl], st, g_fin], writes=[ob])
                cx.dma("pool", out[s, tt * 128:(tt + 1) * 128, :], ob.ap, reads=[ob], writes=[], semof=ob)

            if n + 1 < NCH:
                pending_final = [(lambda t=t_, f=final_tile: f(t)) for t_ in range(4)]
            else:
                for t_ in range(4):
                    final_tile(t_)

    for ob in outt:
        for k, t in ob.rds.items():
            cx.wait("pool", t)
    print(f"[kernel] SBUF arena: A={arenaBC - base0} B={endB - arenaBC} C={endC - arenaBC} top={top - base0}")
    return nc


def _freq_tab():
    inv = (np.float32(10000.0) ** (-(np.arange(0, 32, 2, dtype=np.float32)) / np.float32(32))).astype(np.float32)
    tab = np.zeros((128, 4), np.float32)
    tab[64:80, 0] = inv
    tab[80:96, 0] = inv
    tab[64:96, 1] = np.float32(math.pi / 2)
    tab[64:80, 2] = -inv
    tab[80:96, 2] = inv
    return tab


def _in_map(xs, ps, inputs):
    f = lambda a: np.ascontiguousarray(np.asarray(a, dtype=np.float32))
    return {
        "x": f(xs), "positions": np.ascontiguousarray(np.asarray(ps, dtype=np.int32)),
        "norm_mix_g": f(inputs["norm_mix_g"]).reshape(1, D),
        "w_in": f(inputs["w_in"]).reshape(D, 1960),
        "b_fgate": f(inputs["b_fgate"]).reshape(8, 1),
        "q_norm_g": f(inputs["q_norm_g"]).reshape(256, 1),
        "w_uq": f(inputs["w_uq"]).reshape(256, 768),
        "kv_norm_g": f(inputs["kv_norm_g"]).reshape(128, 1),
        "w_ukv": f(inputs["w_ukv"]).reshape(128, 1024),
        "fox_out_g": f(inputs["fox_out_g"]).reshape(1, 512),
        "mla_out_g": f(inputs["mla_out_g"]).reshape(1, 512),
        "w_o": f(inputs["w_o"]).reshape(D, D),
        "norm_ffn_g": f(inputs["norm_ffn_g"]).reshape(1, D),
        "w_gate": f(inputs["w_gate"]).reshape(D, DFF),
        "w_up": f(inputs["w_up"]).reshape(D, DFF),
        "w_down": f(inputs["w_down"]).reshape(DFF, D),
        "final_norm_g": f(inputs["final_norm_g"]).reshape(1, D),
        "freq_tab": _freq_tab(),
    }


def kernel(**inputs):
    ncores = 8
    x = np.asarray(inputs["x"])
    p = np.asarray(inputs["positions"])
    nseq = x.shape[0] // ncores
    nc = build_nc(nseq)
    in_maps = [_in_map(x[i * nseq:(i + 1) * nseq], p[i * nseq:(i + 1) * nseq], inputs) for i in range(ncores)]
    res = run_bass_kernel_spmd(nc, in_maps, core_ids=list(range(ncores)))
    return np.concatenate([np.asarray(r["out"]) for r in res.results], axis=0).astype(np.float32)
```

```python
import math
from contextlib import contextmanager
import numpy as np
import concourse.bass as bass
import concourse.mybir as mybir
from concourse.bass_utils import run_bass_kernel_spmd

F32 = mybir.dt.float32
BF16 = mybir.dt.bfloat16
I32 = mybir.dt.int32
AF = mybir.ActivationFunctionType
ALU = mybir.AluOpType

S = 2048
D = 1024
NT = S // 128
NCH = S // 512
DFF = 2816
NFC = DFF // 128
EPS = 1e-6
C_F = 1536
C_QL = 1544
C_KV = 1800
C_KR = 1928
WIN = 1992
TWO_PI = 2.0 * math.pi
CW1 = 6.28125
CW2 = TWO_PI - CW1


class Buf:
    registry = []

    def __init__(self, name, ap, excl=False, parent=None):
        self.name = name
        self.ap = ap
        self.excl = excl
        self.rng = parent.rng if parent is not None else None
        self.arena = parent.arena if parent is not None else None
        self.alias = []
        Buf.registry.append(self)
        self.wr = None
        self.rds = {}
        self.dsem = None
        self.dcount = 0


class Ctx:
    def __init__(self, nc):
        self.nc = nc
        self.eng = {"pe": nc.tensor, "act": nc.scalar, "dve": nc.vector, "pool": nc.gpsimd, "sp": nc.sync}
        self.psem = {k: nc.alloc_semaphore("prog_" + k) for k in ("pe", "act", "dve", "pool")}
        self.pcount = {k: 0 for k in self.psem}
        self.waited = {}
        self.dma_last = {}

    def wait(self, e, tok):
        if tok is None:
            return
        sem, val, key = tok
        if e == "pe" and key == "pe":
            return
        if self.waited.get((e, key), 0) >= val:
            return
        self.eng[e].wait_ge(sem, val)
        self.waited[(e, key)] = val

    def deps(self, e, reads, writes):
        for b in reads:
            self.wait(e, b.wr)
            if b.excl:
                for k, t in b.rds.items():
                    if k != e:
                        self.wait(e, t)
        for b in writes:
            self.wait(e, b.wr)
            for k, t in b.rds.items():
                self.wait(e, t)
            for a in b.alias:
                self.wait(e, a.wr)
                for k, t in a.rds.items():
                    self.wait(e, t)

    def done(self, e, inst, reads, writes):
        self.pcount[e] += 1
        inst.then_inc(self.psem[e], 1)
        tok = (self.psem[e], self.pcount[e], e)
        for b in reads:
            b.rds[e] = tok
        for b in writes:
            b.wr = tok
            b.rds = {}
        return tok

    def op(self, e, fn, reads=(), writes=()):
        self.deps(e, reads, writes)
        inst = fn(self.eng[e])
        return self.done(e, inst, reads, writes)

    @contextmanager
    def grp(self, e, reads=(), writes=()):
        self.deps(e, reads, writes)
        h = [None]
        yield h
        self.done(e, h[0], reads, writes)

    def dma(self, q, out, in_, reads=(), writes=(), semof=None):
        self.deps(q, reads, writes)
        sb = semof if semof is not None else writes[0]
        if sb.dsem is None:
            sb.dsem = self.nc.alloc_semaphore("d_" + sb.name)
        inst = self.eng[q].dma_start(out=out, in_=in_)
        sb.dcount += 16
        inst.then_inc(sb.dsem, 16)
        key = "d_" + sb.name
        tok = (sb.dsem, sb.dcount, key)
        self.dma_last[key] = tok
        for b in reads:
            b.rds[key] = tok
        for b in writes:
            b.wr = tok
            b.rds = {}
        return tok

    def barrier(self, dummy):
        for e in ("pe", "act", "dve"):
            if self.pcount[e]:
                self.wait("pool", (self.psem[e], self.pcount[e], e))
        for tok in self.dma_last.values():
            self.wait("pool", tok)
        tok = self.op("pool", lambda g: g.memset(dummy.ap, 0.0), writes=[dummy])
        for e in ("pe", "act", "dve", "sp"):
            self.wait(e, tok)


def build_nc(nseq):
    nc = bass.Bass("TRN2", target_bir_lowering=False)
    cx = Ctx(nc)
    T, A, V, G = nc.tensor, nc.scalar, nc.vector, nc.gpsimd

    def din(name, shape, dt=F32):
        return nc.dram_tensor(name, list(shape), dt, kind="ExternalInput").ap()

    x = din("x", [nseq, S, D])
    pos = din("positions", [nseq, S], I32)
    g_mix_d = din("norm_mix_g", [1, D])
    w_in_d = din("w_in", [D, 1960])
    bf_d = din("b_fgate", [8, 1])
    qg_d = din("q_norm_g", [256, 1])
    w_uq_d = din("w_uq", [256, 768])
    kvg_d = din("kv_norm_g", [128, 1])
    w_ukv_d = din("w_ukv", [128, 1024])
    g_fox_d = din("fox_out_g", [1, 512])
    g_mla_d = din("mla_out_g", [1, 512])
    w_o_d = din("w_o", [D, D])
    g_ffn_d = din("norm_ffn_g", [1, D])
    w_g_d = din("w_gate", [D, DFF])
    w_u_d = din("w_up", [D, DFF])
    w_d_d = din("w_down", [DFF, D])
    g_fin_d = din("final_norm_g", [1, D])
    freq_d = din("freq_tab", [128, 4])
    out = nc.dram_tensor("out", [nseq, S, D], F32, kind="ExternalOutput").ap()

    def dscr(name, shape):
        return nc.dram_tensor(name, list(shape), BF16, kind="Internal").ap()

    win_b = dscr("win_b", [D, WIN])
    wo_b = dscr("wo_b", [D, D])
    wgu_b = dscr("wgu_b", [NFC, 128, 2, 8, 128])
    wd_b = dscr("wd_b", [DFF, D])

    base0 = (nc.sbuf_base + 63) // 64 * 64
    top = nc.sbuf_top
    cur = [base0]

    Buf.registry = []
    arena = ["A"]

    def sb(name, shape, dt, excl=False):
        nbytes = int(np.prod(shape[1:])) * (4 if dt in (F32, I32) else 2)
        off = cur[0]
        cur[0] = (off + nbytes + 63) // 64 * 64
        assert cur[0] <= top, f"SBUF overflow at {name}: {cur[0]} > {top}"
        t = nc.alloc_sbuf_tensor_at(name, list(shape), dt, offset=off)
        b = Buf(name, t.ap())
        b.rng = (off, off + nbytes)
        b.arena = arena[0]
        return b

    O_all = sb("O_all", [128, NT, 1024], BF16)
    ident = sb("ident", [128, 128], BF16)
    maskT = sb("maskT", [128, 128], BF16)
    ones_bf = sb("ones_bf", [128, 128], BF16)
    wuqm = sb("wuqm", [128, 2, 8, 128], BF16)
    wukv = sb("wukv", [128, 1024], BF16)
    freq = sb("freq", [128, 4], F32)
    negb = sb("negb", [128, 1], F32)
    qg = sb("qg", [128, 2], F32)
    kvg = sb("kvg", [128, 1], F32)
    ones8 = sb("ones8", [128, 1], F32)
    dummy = sb("dummy", [128, 1], F32)
    stats = [sb(f"st{i}", [128, 2], F32) for i in range(8)]
    rrs = [sb(f"rr{i}", [128, 4], F32) for i in range(2)]
    rkv_tok = sb("rkv_tok", [128, NT], F32)
    sqj = [sb(f"sqj{i}", [128, 512], BF16) for i in range(2)]
    arenaBC = cur[0]

    arena[0] = "B"
    hT = sb("hT", [128, 8, S], BF16)
    hTc = [Buf(f"hTc{i}", hT.ap, parent=hT) for i in range(NCH)]
    xin = [sb(f"xin{i}", [128, 1024], F32) for i in range(3)]
    hbf = [sb(f"hbf{i}", [128, 1024], BF16) for i in range(2)]
    g_mix = sb("g_mix", [128, 1024], F32)
    w_in = sb("w_in", [128, 8, WIN], BF16)
    ropeS = sb("ropeS", [128, 1024], F32)
    w_qk = Buf("w_qk", w_in.ap, parent=w_in)
    w_vf = Buf("w_vf", w_in.ap, parent=w_in)
    w_lat = Buf("w_lat", w_in.ap, parent=w_in)
    w_kr = Buf("w_kr", w_in.ap, parent=w_in)
    TMP = sb("TMP", [128, S], F32)
    T1 = sb("T1", [128, S], F32)
    T2 = sb("T2", [128, S], F32)
    q_latT = sb("q_latT", [128, 2, S], BF16)
    kv_latT = sb("kv_latT", [128, S], BF16)
    k_peT = sb("k_peT", [128, S], BF16)
    QT = [sb(f"QT{i}", [128, S], BF16) for i in range(2)]
    KT = [sb(f"KT{i}", [128, S], BF16) for i in range(2)]
    QTa = [Buf(f"QTa{i}", QT[i].ap, parent=QT[i]) for i in range(2)]
    KTa = [Buf(f"KTa{i}", KT[i].ap, parent=KT[i]) for i in range(2)]
    V_all = sb("V_all", [128, NT, 8, 66], BF16)
    PT = [sb(f"PT{i}", [128, 512], BF16) for i in range(4)]
    sq = [Buf(f"sq{i}", hbf[i].ap[:, 0:512], parent=hbf[i]) for i in range(2)]
    T1h = Buf("T1h", T1.ap, parent=T1)
    T2h = Buf("T2h", T2.ap, parent=T2)
    TMPh = Buf("TMPh", TMP.ap, parent=TMP)
    endB = cur[0]

    cur[0] = arenaBC
    arena[0] = "C"
    w_o = sb("w_o", [128, 8, 1024], BF16)
    g_ffn = sb("g_ffn", [128, 1024], F32)
    OT0 = sb("OT0", [128, 8, 512], BF16)
    xr = [sb(f"xr{i}", [128, 1024], F32) for i in range(2)]
    x1_0 = [sb(f"x1_0_{i}", [128, 1024], F32) for i in range(4)]
    hbf2 = [sb(f"hbf2_{i}", [128, 1024], BF16) for i in range(2)]
    h2T0 = sb("h2T0", [128, 8, 512], BF16)
    g_foxC = sb("g_foxC", [128, 512], F32)
    g_mlaC = sb("g_mlaC", [128, 512], F32)
    NWGU, NWD = 4, 6
    wgu = [sb(f"wgu{i}", [128, 2, 8, 128], BF16) for i in range(NWGU)]
    g_fin = sb("g_fin", [128, 1024], F32)
    OT = [OT0, sb("OT1", [128, 8, 512], BF16)]
    h2T = [h2T0, sb("h2T1", [128, 8, 512], BF16)]
    x1 = [x1_0, [sb(f"x1_1_{i}", [128, 1024], F32) for i in range(4)]]
    aT = [sb(f"aT{i}", [128, 512], BF16) for i in range(NFC)]
    wd = [sb(f"wd{i}", [128, 2, 512], BF16) for i in range(NWD)]
    sg = [sb(f"sg{i}", [128, 512], BF16) for i in range(2)]
    outt = [sb(f"outt{i}", [128, 1024], F32) for i in range(2)]
    endC = cur[0]

    PS = [Buf(f"ps{i}", nc.alloc_psum_tensor(f"ps{i}", [128, 512], F32).ap(), excl=True) for i in range(8)]

    for b1 in Buf.registry:
        if b1.arena == "B":
            for b2 in Buf.registry:
                if b2.arena == "C" and b1.rng[0] < b2.rng[1] and b2.rng[0] < b1.rng[1]:
                    b1.alias.append(b2)
                    b2.alias.append(b1)

    win_bB = Buf("win_bB", win_b)
    win_qkB = Buf("win_qkB", win_b)
    wo_bB = Buf("wo_bB", wo_b)
    wgu_bB = Buf("wgu_bB", wgu_b)
    wd_bB = Buf("wd_bB", wd_b)

    stat_i = [0]

    def new_stat():
        stat_i[0] += 1
        return stats[stat_i[0] % len(stats)]

    cx.op("pool", lambda g: g.memset(ident.ap, 0.0), writes=[ident])
    cx.op("pool", lambda g: g.affine_select(out=ident.ap, in_=ident.ap, pattern=[[-1, 128]],
                                            compare_op=ALU.not_equal, fill=1.0, base=0,
                                            channel_multiplier=1), reads=[ident], writes=[ident])
    cx.op("pool", lambda g: g.memset(maskT.ap, 0.0), writes=[maskT])
    cx.op("pool", lambda g: g.affine_select(out=maskT.ap, in_=maskT.ap, pattern=[[1, 128]],
                                            compare_op=ALU.is_ge, fill=-30000.0, base=0,
                                            channel_multiplier=-1), reads=[maskT], writes=[maskT])
    cx.op("pool", lambda g: g.memset(ones_bf.ap, 1.0), writes=[ones_bf])
    cx.op("pool", lambda g: g.memset(ones8.ap, 1.0), writes=[ones8])
    cx.dma("sp", freq.ap, freq_d, writes=[freq])
    cx.dma("sp", negb.ap[0:8, :], bf_d, writes=[negb])
    for c2 in range(2):
        cx.dma("sp", qg.ap[:, c2:c2 + 1], qg_d[c2 * 128:(c2 + 1) * 128, :], writes=[qg])
    cx.dma("sp", kvg.ap, kvg_d, writes=[kvg])
    cx.op("dve", lambda v: v.tensor_scalar(out=negb.ap[0:8, :], in0=negb.ap[0:8, :], scalar1=-1.0,
                                           scalar2=None, op0=ALU.mult), reads=[negb], writes=[negb])
    cx.dma("pool", win_b[:, 1024:1960], w_in_d[:, 1024:1960], writes=[win_bB])
    cx.dma("pool", win_b[:, 1960:1976], w_in_d[:, C_KR + 16:C_KR + 32], writes=[win_bB])
    cx.dma("pool", win_b[:, 1976:1992], w_in_d[:, C_KR:C_KR + 16], writes=[win_bB])
    win_v0 = win_b.rearrange("(c p) e -> p c e", p=128)
    w0sem = {k: Buf("w0sem_" + k, w_in.ap) for k in ("w_vf", "w_kr", "w_lat", "w_qk")}
    cx.dma("pool", w_in.ap[:, :, 1024:1544], win_v0[:, :, 1024:1544], reads=[win_bB], writes=[w_vf], semof=w0sem["w_vf"])
    cx.dma("pool", w_in.ap[:, :, 1864:WIN], win_v0[:, :, 1864:WIN], reads=[win_bB], writes=[w_kr], semof=w0sem["w_kr"])
    cx.dma("pool", w_in.ap[:, :, 1544:1864], win_v0[:, :, 1544:1864], reads=[win_bB], writes=[w_lat], semof=w0sem["w_lat"])
    for rb in range(8):
        rows = slice(rb * 128, (rb + 1) * 128)
        for j in range(2):
            cx.dma("pool", win_b[rows, 0:1024].rearrange("d (h j e) -> d h j e", h=8, j=2)[:, :, j, :],
                   w_in_d[rows, j * 512:(j + 1) * 512].rearrange("d (h e) -> d h e", h=8), writes=[win_qkB])
    cx.dma("pool", w_in.ap[:, :, 0:1024], win_v0[:, :, 0:1024], reads=[win_qkB], writes=[w_qk], semof=w0sem["w_qk"])
    wuq_v = w_uq_d.rearrange("(c p) (h e) -> p c h e", p=128, h=8)
    for c2 in range(2):
        cx.dma("pool", wuqm.ap[:, c2, :, 0:96], wuq_v[:, c2, :, 0:96], writes=[wuqm])
        cx.dma("pool", wuqm.ap[:, c2, :, 96:112], wuq_v[:, c2, :, 80:96], writes=[wuqm])
        cx.dma("pool", wuqm.ap[:, c2, :, 112:128], wuq_v[:, c2, :, 64:80], writes=[wuqm])
    cx.dma("pool", wukv.ap, w_ukv_d, writes=[wukv])
    deferred = []
    deferred.append(lambda: cx.dma("pool", wo_b, w_o_d, writes=[wo_bB]))
    wg_v = w_g_d.rearrange("(c p) (fc f) -> fc p c f", p=128, f=128)
    wu_v = w_u_d.rearrange("(c p) (fc f) -> fc p c f", p=128, f=128)
    for fc in range(NFC):
        deferred.append(lambda fc=fc: cx.dma("pool", wgu_b[fc, :, 0, :, :], wg_v[fc], writes=[wgu_bB]))
        deferred.append(lambda fc=fc: cx.dma("pool", wgu_b[fc, :, 1, :, :], wu_v[fc], writes=[wgu_bB]))
    for r in range(0, DFF, 704):
        deferred.append(lambda r=r: cx.dma("pool", wd_b[r:r + 704, :], w_d_d[r:r + 704, :], writes=[wd_bB]))

    def run_deferred(k):
        for _ in range(k):
            if deferred:
                deferred.pop(0)()

    def rstd_from(st, n):
        cx.op("act", lambda a: a.activation(out=st.ap[:, 1:2], in_=st.ap[:, 0:1], func=AF.Ln,
                                            scale=1.0 / n, bias=EPS), reads=[st], writes=[st])
        cx.op("act", lambda a: a.activation(out=st.ap[:, 1:2], in_=st.ap[:, 1:2], func=AF.Exp,
                                            scale=-0.5), reads=[st], writes=[st])

    def rms_to_bf(xb, xap, gb, hb):
        st = new_stat()
        cx.op("act", lambda a: a.activation(out=hb.ap, in_=xap, func=AF.Square,
                                            accum_out=st.ap[:, 0:1]), reads=[xb], writes=[hb, st])
        rstd_from(st, 1024)
        cx.op("dve", lambda v: v.scalar_tensor_tensor(out=hb.ap, in0=xap, scalar=st.ap[:, 1:2],
                                                      in1=gb.ap, op0=ALU.mult, op1=ALU.mult),
              reads=[xb, st, gb], writes=[hb])

    def transpose_pe(src_b, src_ap, bank):
        psb = bank.ap.bitcast(BF16)
        with cx.grp("pe", reads=[src_b, ident], writes=[bank]) as h:
            for c in range(8):
                h[0] = T.transpose(out=psb[:, c * 128:(c + 1) * 128], in_=src_ap[:, c * 128:(c + 1) * 128],
                                   identity=ident.ap)

    def transpose_evac(bank, dst_b, dst_ap):
        psb = bank.ap.bitcast(BF16)
        cx.op("dve", lambda v: v.tensor_copy(out=dst_ap, in_=psb.rearrange("p (c t) -> p c t", c=8)),
              reads=[bank], writes=[dst_b])

    def transpose_to(src_b, src_ap, bank, dst_b, dst_ap, evac="dve"):
        psb = bank.ap.bitcast(BF16)
        with cx.grp("pe", reads=[src_b, ident], writes=[bank]) as h:
            for c in range(8):
                h[0] = T.transpose(out=psb[:, c * 128:(c + 1) * 128], in_=src_ap[:, c * 128:(c + 1) * 128],
                                   identity=ident.ap)
        if evac == "act":
            cx.op("act", lambda a: a.copy(out=dst_ap, in_=psb.rearrange("p (c t) -> p c t", c=8)),
                  reads=[bank], writes=[dst_b])
        else:
            cx.op("dve", lambda v: v.tensor_copy(out=dst_ap, in_=psb.rearrange("p (c t) -> p c t", c=8)),
                  reads=[bank], writes=[dst_b])

    def attention(QTb, QTab, KTb, KTab, dk, hv, scale, ocol, filler, filler_start=0, filler_rate=2):
        qt, kt_ = QTb.ap, KTb.ap
        steps = []
        for j in range(NCH):
            for kt in range(4 * j + 4):
                steps.append((j, kt))
        nst = len(steps)
        sbank = [PS[0], PS[1], PS[2]]
        obank = [PS[3], PS[4]]

        def qk(n):
            j, kt = steps[n]
            i = kt - 4 * j
            q0 = max(0, i) * 128
            bk = sbank[n % 3]
            with cx.grp("pe", reads=[QTb, QTab, KTb, KTab, ident, maskT], writes=[bk]) as h:
                h[0] = T.matmul(bk.ap[:, q0:512], lhsT=kt_[0:dk, kt * 128:(kt + 1) * 128],
                                rhs=qt[0:dk, j * 512 + q0:(j + 1) * 512], start=True, stop=(i < 0))
                if i >= 0:
                    h[0] = T.matmul(bk.ap[:, q0:q0 + 128], lhsT=ident.ap, rhs=maskT.ap, start=False, stop=True)

        def ex(n):
            j, kt = steps[n]
            q0 = max(0, kt - 4 * j) * 128
            bk = sbank[n % 3]
            pt = PT[n % 4]
            cx.op("act", lambda a: a.activation(out=pt.ap[:, q0:512], in_=bk.ap[:, q0:512], func=AF.Exp,
                                                scale=scale), reads=[bk], writes=[pt])

        def pv(n):
            j, kt = steps[n]
            i = kt - 4 * j
            pt = PT[n % 4]
            ob = obank[j % 2]
            ov = ob.ap[:, 0:260].rearrange("p (a b) -> p a b", a=4)
            with cx.grp("pe", reads=[pt, V_all], writes=[ob]) as h:
                for qb in range(max(0, i), 4):
                    h[0] = T.matmul(ov[:, qb, :], lhsT=pt.ap[:, qb * 128:(qb + 1) * 128],
                                    rhs=V_all.ap[:, kt, hv, 0:65], start=(kt == 0 and qb == 0),
                                    stop=(kt == 4 * j + qb), skip_group_check=True)
            if kt == 4 * j + 3:
                rr = rrs[j % 2]
                cx.op("dve", lambda v: v.reciprocal(out=rr.ap, in_=ov[:, :, 64]), reads=[ob], writes=[rr])
                cx.op("dve", lambda v: v.tensor_tensor(
                    out=O_all.ap[:, 4 * j:4 * j + 4, ocol:ocol + 64], in0=ov[:, :, 0:64],
                    in1=rr.ap.unsqueeze(2).broadcast_to([128, 4, 64]), op=ALU.mult),
                    reads=[ob, rr], writes=[O_all])

        qk(0)
        qk(1)
        for n in range(nst):
            ex(n)
            if n + 2 < nst:
                qk(n + 2)
            pv(n)
            if filler is not None and n >= filler_start:
                for _ in range(filler_rate):
                    try:
                        next(filler)
                    except StopIteration:
                        filler = None
                        break
        if filler is not None:
            for _ in filler:
                pass

    pbank = [PS[5], PS[6]]
    fbank = [PS[5], PS[6], PS[7]]
    pbi = [0]

    def next_pbank():
        pbi[0] += 1
        return pbank[pbi[0] % 2]

    def next_fbank():
        pbi[0] += 1
        return fbank[pbi[0] % 3]

    def fox_proj(h, b):
        cst = T2.ap.bitcast(BF16).rearrange("p (a s) -> p a s", a=2)
        cx.op("pool", lambda g: g.memset(QT[b].ap[64:68, :], -1.0), writes=[QTa[b]])
        cx.op("pool", lambda g: g.memset(KT[b].ap[64:68, :], 1.0), writes=[KTa[b]])
        for r_ in range(2):
            cx.dma("pool", QT[b].ap[64 + r_:65 + r_, :], cst[h:h + 1, r_, :], reads=[T2], writes=[QTa[b]])
            cx.dma("pool", KT[b].ap[66 + r_:67 + r_, :], cst[h:h + 1, r_, :], reads=[T2], writes=[KTa[b]])
        yield
        for n in range(NCH):
            bk = next_fbank()
            with cx.grp("pe", reads=[w_qk] + hTc, writes=[bk]) as hh:
                for c in range(8):
                    hh[0] = T.matmul(bk.ap, lhsT=w_in.ap[:, c, h * 128:(h + 1) * 128],
                                     rhs=hT.ap[:, c, n * 512:(n + 1) * 512], start=(c == 0), stop=(c == 7))
                    if c == 3:
                        yield
            cx.op("dve", lambda v: v.tensor_copy(out=QT[b].ap[0:64, n * 512:(n + 1) * 512], in_=bk.ap[0:64, :]),
                  reads=[bk], writes=[QT[b]])
            cx.op("dve", lambda v: v.tensor_copy(out=KT[b].ap[0:64, n * 512:(n + 1) * 512], in_=bk.ap[64:128, :]),
                  reads=[bk], writes=[KT[b]])
            yield

    rope_i = [0]

    def mla_proj(h, b):
        cx.op("pool", lambda g: g.tensor_copy(out=KT[b].ap[64:96, :], in_=k_peT.ap[64:96, :]),
              reads=[k_peT], writes=[KTa[b]])
        yield
        if 3 <= h <= 6:
            group_norm_tile(0, h - 3, g_foxC)
            yield
        for n in range(NCH):
            cs = slice(n * 512, (n + 1) * 512)
            rb = ropeS
            rA, rB = rb.ap[64:96, 0:512], rb.ap[64:96, 512:1024]
            bq = next_fbank()
            with cx.grp("pe", reads=[wuqm, q_latT], writes=[bq]) as hh:
                for c2 in range(2):
                    hh[0] = T.matmul(bq.ap, lhsT=wuqm.ap[:, c2, h, :],
                                     rhs=q_latT.ap[:, c2, cs], start=(c2 == 0), stop=(c2 == 1))
            cx.op("dve", lambda v: v.tensor_tensor(out=QT[b].ap[0:64, cs], in0=bq.ap[0:64, :],
                                                   in1=TMP.ap[0:64, cs], op=ALU.mult),
                  reads=[bq, TMP], writes=[QT[b]])
            yield
            cx.op("dve", lambda v: v.tensor_tensor(out=rA, in0=bq.ap[64:96, :],
                                                   in1=T1.ap[64:96, cs], op=ALU.mult),
                  reads=[bq, T1h], writes=[rb])
            cx.op("dve", lambda v: v.tensor_tensor(out=rB, in0=bq.ap[96:128, :],
                                                   in1=T2.ap[64:96, cs], op=ALU.mult),
                  reads=[bq, T2h, rb], writes=[rb])
            cx.op("pool", lambda g: g.tensor_tensor(out=QT[b].ap[64:96, cs], in0=rA, in1=rB, op=ALU.add),
                  reads=[rb], writes=[QTa[b]])
            yield
            bk = next_fbank()
            with cx.grp("pe", reads=[wukv, kv_latT], writes=[bk]) as hh:
                hh[0] = T.matmul(bk.ap[0:64, :], lhsT=wukv.ap[:, h * 128:h * 128 + 64],
                                 rhs=kv_latT.ap[:, cs], start=True, stop=True)
            cx.op("dve", lambda v: v.tensor_tensor(out=KT[b].ap[0:64, cs], in0=bk.ap[0:64, :],
                                                   in1=T2.ap[0:64, cs], op=ALU.mult),
                  reads=[bk, T2], writes=[KT[b]])
            yield

    def drain(gen):
        for _ in gen:
            pass

    def group_norm_tile(gi, tt, gb):
        oap = O_all.ap[:, tt, gi * 512:(gi + 1) * 512]
        st = new_stat()
        jb = sqj[tt % 2]
        cx.op("act", lambda a: a.activation(out=jb.ap, in_=oap, func=AF.Square, accum_out=st.ap[:, 0:1]),
              reads=[O_all], writes=[jb, st])
        rstd_from(st, 512)
        cx.op("dve", lambda v: v.scalar_tensor_tensor(out=oap, in0=oap, scalar=st.ap[:, 1:2], in1=gb.ap,
                                                      op0=ALU.mult, op1=ALU.mult),
              reads=[O_all, st, gb], writes=[O_all])

    def group_norm(gi, gb):
        for tt in range(NT):
            group_norm_tile(gi, tt, gb)

    win_v = win_b.rearrange("(c p) e -> p c e", p=128)

    def load_w(buf, c0, c1):
        src = win_qkB if c0 == 0 else win_bB
        cx.dma("sp", w_in.ap[:, :, c0:c1], win_v[:, :, c0:c1], reads=[src], writes=[buf])

    def norm1_gen(sq_, tbanks):
        cx.dma("sp", g_mix.ap, g_mix_d.broadcast_to([128, D]), writes=[g_mix])

        def xload(t):
            cx.dma("sp", xin[t % 3].ap, x[sq_, t * 128:(t + 1) * 128, :], writes=[xin[t % 3]])

        xload(0)
        xload(1)
        yield
        for tt in range(NT + 2):
            if tt + 2 < NT:
                xload(tt + 2)
            if tt < NT:
                rms_to_bf(xin[tt % 3], xin[tt % 3].ap, g_mix, hbf[tt % 2])
            if 1 <= tt <= NT:
                t = tt - 1
                transpose_pe(hbf[t % 2], hbf[t % 2].ap, tbanks[t % 2])
            if 2 <= tt <= NT + 1:
                t = tt - 2
                transpose_evac(tbanks[t % 2], hTc[t // 4], hT.ap[:, :, t * 128:(t + 1) * 128])
            if sq_ > 0:
                if tt == 0:
                    load_w(w_vf, 1024, 1544)
                if tt == 2:
                    load_w(w_kr, 1864, WIN)
                if tt == 6:
                    load_w(w_qk, 0, 1024)
                if tt == 10:
                    load_w(w_lat, 1544, 1864)
            yield

    wgu_i = [0]
    wd_i = [0]
    for s in range(nseq):
        T1i = T1.ap.bitcast(I32)

        def fox_f_chunk(n):
            bk = next_pbank()
            with cx.grp("pe", reads=[w_vf, hTc[n]], writes=[bk]) as hh:
                for c in range(8):
                    hh[0] = T.matmul(bk.ap[0:8, :], lhsT=w_in.ap[:, c, C_F:C_F + 8],
                                     rhs=hT.ap[:, c, n * 512:(n + 1) * 512], start=(c == 0), stop=(c == 7))
            cx.op("act", lambda a: a.activation(out=TMP.ap[0:8, n * 512:(n + 1) * 512], in_=bk.ap[0:8, :],
                                                func=AF.Exp, scale=-1.0, bias=negb.ap[0:8, 0:1]),
                  reads=[bk, negb], writes=[TMP])

        def fox_v_tile(tt):
            bk = next_pbank()
            with cx.grp("pe", reads=[w_vf, hTc[tt // 4]], writes=[bk]) as hh:
                for c in range(8):
                    hh[0] = T.matmul(bk.ap, lhsT=hT.ap[:, c, tt * 128:(tt + 1) * 128],
                                     rhs=w_in.ap[:, c, 1024:1536], start=(c == 0), stop=(c == 7))
            cx.op("dve", lambda v: v.tensor_copy(out=V_all.ap[:, tt, :, 0:64],
                                                 in_=bk.ap.rearrange("p (h e) -> p h e", h=8)),
                  reads=[bk], writes=[V_all])

        def build_rope_tables():
            cx.op("dve", lambda v: v.tensor_copy(out=TMP.ap[64:96, :], in_=T1i[64:96, :]), reads=[T1h], writes=[TMPh])
            yield
            for (tab, fcol) in ((T1h, 0), (T2h, 2)):
                cx.op("dve", lambda v: v.tensor_scalar(out=tab.ap[64:96, :], in0=TMP.ap[64:96, :],
                                                       scalar1=freq.ap[64:96, fcol:fcol + 1],
                                                       scalar2=freq.ap[64:96, fcol + 1:fcol + 2],
                                                       op0=ALU.mult, op1=ALU.add),
                      reads=[TMPh, freq], writes=[tab])
                yield
                for half in range(2):
                    hs = slice(half * 1024, (half + 1) * 1024)
                    ni = ropeS.ap.bitcast(I32)
                    cx.op("dve", lambda v: v.tensor_scalar(out=ni[64:96, :], in0=tab.ap[64:96, hs],
                                                           scalar1=1.0 / TWO_PI, scalar2=None, op0=ALU.mult),
                          reads=[tab], writes=[ropeS])
                    yield
                    for cw in (CW1, CW2):
                        cx.op("dve", lambda v: v.scalar_tensor_tensor(out=tab.ap[64:96, hs], in0=ni[64:96, :],
                                                                      scalar=-cw, in1=tab.ap[64:96, hs],
                                                                      op0=ALU.mult, op1=ALU.add),
                              reads=[ropeS, tab], writes=[tab])
                        yield
                cx.op("dve", lambda v: v.tensor_scalar(out=tab.ap[64:96, :], in0=tab.ap[64:96, :], scalar1=3.1415925,
                                                       scalar2=-3.1415925, op0=ALU.min, op1=ALU.max),
                      reads=[tab], writes=[tab])
                yield
                cx.op("act", lambda a: a.activation(out=tab.ap[64:96, :], in_=tab.ap[64:96, :], func=AF.Sin),
                      reads=[tab], writes=[tab])

        def kpe_chunk(n):
            cs = slice(n * 512, (n + 1) * 512)
            rA, rB = ropeS.ap[64:96, 0:512], ropeS.ap[64:96, 512:1024]
            ba = next_pbank()
            with cx.grp("pe", reads=[w_kr, hTc[n]], writes=[ba]) as hh:
                for c in range(8):
                    hh[0] = T.matmul(ba.ap, lhsT=w_in.ap[:, c, C_KR - 64:C_KR + 64],
                                     rhs=hT.ap[:, c, cs], start=(c == 0), stop=(c == 7))
            cx.op("dve", lambda v: v.tensor_tensor(out=rA, in0=ba.ap[64:96, :], in1=T1.ap[64:96, cs], op=ALU.mult),
                  reads=[ba, T1h], writes=[ropeS])
            cx.op("dve", lambda v: v.tensor_tensor(out=rB, in0=ba.ap[96:128, :], in1=T2.ap[64:96, cs], op=ALU.mult),
                  reads=[ba, T2h, ropeS], writes=[ropeS])
            cx.op("pool", lambda g: g.tensor_tensor(out=k_peT.ap[64:96, cs], in0=rA, in1=rB, op=ALU.add),
                  reads=[ropeS], writes=[k_peT])

        if s == 0:
            drain(norm1_gen(0, (PS[3], PS[4])))
        T1i = T1.ap.bitcast(I32)
        cx.dma("sp", T1i[64:96, :], pos[s:s + 1, :].broadcast_to([32, S]), writes=[T1h])
        rt = build_rope_tables()
        for n in range(NCH):
            fox_f_chunk(n)
            for t4 in range(4):
                fox_v_tile(4 * n + t4)
                for _ in range(2):
                    next(rt, None)
        drain(rt)
        for n in range(NCH):
            kpe_chunk(n)
        cx.op("pool", lambda g: g.memset(V_all.ap[:, :, :, 64:65], 1.0), writes=[V_all])
        cx.op("act", lambda a: a.activation(out=TMP.ap[0:8, :], in_=TMP.ap[0:8, :], func=AF.Ln, bias=1.0),
              reads=[TMP], writes=[TMP])
        cx.op("dve", lambda v: v.tensor_tensor_scan(out=T1.ap[0:8, :], data0=ones8.ap[0:8, 0:1].broadcast_to([8, S]),
                                                    data1=TMP.ap[0:8, :], initial=0.0, op0=ALU.mult, op1=ALU.add),
              reads=[TMP, ones8], writes=[T1])
        cst = T2.ap.bitcast(BF16).rearrange("p (a s) -> p a s", a=2)
        cx.op("dve", lambda v: v.tensor_scalar(out=cst[0:8, 0, :], in0=T1.ap[0:8, :], scalar1=-8.0, scalar2=None,
                                               op0=ALU.mult), reads=[T1], writes=[T2])
        cx.op("dve", lambda v: v.scalar_tensor_tensor(out=cst[0:8, 1, :], in0=T1.ap[0:8, :], scalar=-8.0,
                                                      in1=cst[0:8, 0, :], op0=ALU.mult, op1=ALU.subtract),
              reads=[T1, T2], writes=[T2])

        drain(fox_proj(0, 0))
        for h in range(8):
            filler = fox_proj(h + 1, (h + 1) % 2) if h < 7 else None
            attention(QT[h % 2], QTa[h % 2], KT[h % 2], KTa[h % 2], 68, h, 0.125, h * 64, filler)
            run_deferred(5)

        for n in range(NCH):
            cs = slice(n * 512, (n + 1) * 512)
            ssq = PS[7]
            for c2 in range(2):
                bk = next_pbank()
                with cx.grp("pe", reads=[w_lat] + hTc, writes=[bk]) as hh:
                    for c in range(8):
                        hh[0] = T.matmul(bk.ap, lhsT=w_in.ap[:, c, C_QL + c2 * 128:C_QL + (c2 + 1) * 128],
                                         rhs=hT.ap[:, c, cs], start=(c == 0), stop=(c == 7))
                sqb = sq[c2]
                cx.op("act", lambda a: a.activation(out=sqb.ap, in_=bk.ap, func=AF.Square), reads=[bk], writes=[sqb])
                cx.op("dve", lambda v: v.tensor_scalar(out=q_latT.ap[:, c2, cs], in0=bk.ap, scalar1=qg.ap[:, c2:c2 + 1],
                                                       scalar2=None, op0=ALU.mult), reads=[bk, qg], writes=[q_latT])
                with cx.grp("pe", reads=[sqb, ones_bf], writes=[ssq]) as hh:
                    hh[0] = T.matmul(ssq.ap, lhsT=ones_bf.ap, rhs=sqb.ap, start=(c2 == 0), stop=(c2 == 1))
            cx.op("act", lambda a: a.activation(out=TMP.ap[0:96, cs], in_=ssq.ap[0:96, :], func=AF.Ln,
                                                scale=1.0 / 256, bias=EPS), reads=[ssq], writes=[TMP, TMPh])
            cx.op("act", lambda a: a.activation(out=TMP.ap[0:96, cs], in_=TMP.ap[0:96, cs], func=AF.Exp, scale=-0.5),
                  reads=[TMP, TMPh], writes=[TMP, TMPh])
        for tab in (T1h, T2h):
            cx.op("dve", lambda v: v.tensor_tensor(out=tab.ap[64:96, :], in0=tab.ap[64:96, :], in1=TMP.ap[64:96, :],
                                                   op=ALU.mult), reads=[tab, TMPh, k_peT], writes=[tab])
        for n in range(NCH):
            cs = slice(n * 512, (n + 1) * 512)
            ssq = PS[7]
            bk = next_pbank()
            with cx.grp("pe", reads=[w_lat, w_kr] + hTc, writes=[bk]) as hh:
                for c in range(8):
                    hh[0] = T.matmul(bk.ap, lhsT=w_in.ap[:, c, C_KV:C_KV + 128], rhs=hT.ap[:, c, cs],
                                     start=(c == 0), stop=(c == 7))
            sqb = sq[n % 2]
            cx.op("act", lambda a: a.activation(out=sqb.ap, in_=bk.ap, func=AF.Square), reads=[bk], writes=[sqb])
            cx.op("dve", lambda v: v.tensor_scalar(out=kv_latT.ap[:, cs], in0=bk.ap, scalar1=kvg.ap[:, 0:1],
                                                   scalar2=None, op0=ALU.mult), reads=[bk, kvg], writes=[kv_latT])
            with cx.grp("pe", reads=[sqb, ones_bf], writes=[ssq]) as hh:
                hh[0] = T.matmul(ssq.ap, lhsT=ones_bf.ap, rhs=sqb.ap, start=True, stop=True)
            cx.op("act", lambda a: a.activation(out=T2.ap[0:64, cs], in_=ssq.ap[0:64, :], func=AF.Ln,
                                                scale=1.0 / 128, bias=EPS), reads=[ssq], writes=[T2])
            cx.op("act", lambda a: a.activation(out=T2.ap[0:64, cs], in_=T2.ap[0:64, cs], func=AF.Exp, scale=-0.5),
                  reads=[T2], writes=[T2])
            tb = next_pbank()
            with cx.grp("pe", reads=[sqb, ones_bf], writes=[tb]) as hh:
                for ttl in range(4):
                    hh[0] = T.matmul(tb.ap[:, ttl:ttl + 1], lhsT=sqb.ap[:, ttl * 128:(ttl + 1) * 128],
                                     rhs=ones_bf.ap[:, 0:1], start=True, stop=True, skip_group_check=True)
            cx.op("act", lambda a: a.activation(out=rkv_tok.ap[:, 4 * n:4 * n + 4], in_=tb.ap[:, 0:4], func=AF.Ln,
                                                scale=1.0 / 128, bias=EPS), reads=[tb], writes=[rkv_tok])
        cx.op("act", lambda a: a.activation(out=rkv_tok.ap, in_=rkv_tok.ap, func=AF.Exp, scale=-0.5),
              reads=[rkv_tok], writes=[rkv_tok])
        wv = wukv.ap.rearrange("p (h e) -> p h e", h=8)[:, :, 64:128]
        for tt in range(NT):
            bk = next_pbank()
            with cx.grp("pe", reads=[wukv, kv_latT], writes=[bk]) as hh:
                hh[0] = T.matmul(bk.ap.rearrange("p (h e) -> p h e", h=8), lhsT=kv_latT.ap[:, tt * 128:(tt + 1) * 128],
                                 rhs=wv, start=True, stop=True)
            cx.op("dve", lambda v: v.tensor_scalar(out=V_all.ap[:, tt, :, 0:64],
                                                   in0=bk.ap.rearrange("p (h e) -> p h e", h=8),
                                                   scalar1=rkv_tok.ap[:, tt:tt + 1], scalar2=None, op0=ALU.mult),
                  reads=[bk, rkv_tok], writes=[V_all])
        def load_wgu(fc):
            slot = wgu[wgu_i[0] % NWGU]
            wgu_i[0] += 1
            cx.dma("sp", slot.ap, wgu_b[fc], reads=[wgu_bB], writes=[slot])
            return slot

        def load_wd(dh, f2):
            slot = wd[wd_i[0] % NWD]
            wd_i[0] += 1
            cx.dma("sp", slot.ap, wd_b[f2 * 256:(f2 + 1) * 256, dh * 512:(dh + 1) * 512].rearrange(
                "(a p) e -> p a e", p=128), reads=[wd_bB], writes=[slot])
            return slot

        def stageA(n, p, obk=(PS[0], PS[1]), tbk=(PS[2], PS[3]), do_gn=False):
            ot, h2, xx = OT[p], h2T[p], x1[p]

            def gn(ttl):
                group_norm_tile(1, 4 * n + ttl, g_mlaC)

            def trO(ttl):
                transpose_to(O_all, O_all.ap[:, 4 * n + ttl, :], tbk[ttl % 2], ot, ot.ap[:, :, ttl * 128:(ttl + 1) * 128])

            def xload(ttl):
                tt = 4 * n + ttl
                cx.dma("sp", xr[ttl % 2].ap, x[s, tt * 128:(tt + 1) * 128, :], writes=[xr[ttl % 2]])

            def oproj(ttl, dh):
                bk = obk[dh]
                xb = xr[ttl % 2]
                with cx.grp("pe", reads=[ot, w_o], writes=[bk]) as hh:
                    for c in range(8):
                        hh[0] = T.matmul(bk.ap, lhsT=ot.ap[:, c, ttl * 128:(ttl + 1) * 128],
                                         rhs=w_o.ap[:, c, dh * 512:(dh + 1) * 512], start=(c == 0), stop=(c == 7))
                cx.op("dve", lambda v: v.tensor_tensor(out=xx[ttl].ap[:, dh * 512:(dh + 1) * 512], in0=bk.ap,
                                                       in1=xb.ap[:, dh * 512:(dh + 1) * 512], op=ALU.add),
                      reads=[bk, xb], writes=[xx[ttl]])

            def nrm(ttl):
                rms_to_bf(xx[ttl], xx[ttl].ap, g_ffn, hbf2[ttl % 2])

            def trH(ttl):
                hb = hbf2[ttl % 2]
                transpose_to(hb, hb.ap, tbk[ttl % 2], h2, h2.ap[:, :, ttl * 128:(ttl + 1) * 128])

            pre = [[(gn, 0), (xload, 0)], [(gn, 1), (xload, 1)], [(gn, 2)], [(gn, 3)], [(trO, 0)]] if do_gn else \
                [[(xload, 0)], [(trO, 0), (xload, 1)]]
            steps = pre + [
                [(trO, 1)], [(trO, 2)], [(trO, 3)],
                [(oproj, 0, 0)], [(oproj, 0, 1), (nrm, 0)],
                [(oproj, 1, 0), (xload, 2)], [(oproj, 1, 1), (nrm, 1), (trH, 0)],
                [(oproj, 2, 0), (xload, 3)], [(oproj, 2, 1), (nrm, 2), (trH, 1)],
                [(oproj, 3, 0)], [(oproj, 3, 1), (nrm, 3), (trH, 2)], [], [(trH, 3)],
            ]
            for st_ in steps:
                for f_ in st_:
                    f_[0](*f_[1:])
                yield

        cx.dma("sp", w_o.ap, wo_b.rearrange("(c p) e -> p c e", p=128), reads=[wo_bB], writes=[w_o])
        cx.dma("sp", g_ffn.ap, g_ffn_d.broadcast_to([128, D]), writes=[g_ffn])
        cx.dma("sp", g_mlaC.ap, g_mla_d.broadcast_to([128, 512]), writes=[g_mlaC])
        cx.dma("sp", g_foxC.ap, g_fox_d.broadcast_to([128, 512]), writes=[g_foxC])
        drain(mla_proj(0, 0))
        mscale = 96.0 ** -0.5
        for h in range(8):
            if h < 7:
                attention(QT[h % 2], QTa[h % 2], KT[h % 2], KTa[h % 2], 96, h, mscale, 512 + h * 64,
                          mla_proj(h + 1, (h + 1) % 2))
            else:
                attention(QT[h % 2], QTa[h % 2], KT[h % 2], KTa[h % 2], 96, h, mscale, 512 + h * 64,
                          stageA(0, 0, obk=(PS[5], PS[6]), tbk=(PS[7], PS[7]), do_gn=True), filler_start=4, filler_rate=1)
            run_deferred(2)

        run_deferred(100)
        cx.dma("sp", g_fin.ap, g_fin_d.broadcast_to([128, D]), writes=[g_fin])

        next_wq = None
        pending_final = []
        for n in range(NCH):
            p = n % 2
            h2, xx = h2T[p], x1[p]
            wq_slots = next_wq if next_wq else [load_wgu(i_) for i_ in range(NWGU - 1)]
            next_wq = None
            wd_reqs = [(dh_, f2_) for dh_ in range(2) for f2_ in range(NFC // 2)]
            wd_loaded = []

            def issue_wd():
                if len(wd_loaded) < len(wd_reqs):
                    dh_, f2_ = wd_reqs[len(wd_loaded)]
                    wd_loaded.append(load_wd(dh_, f2_))
            for fc in range(NFC):
                slot = wq_slots.pop(0)
                if fc + NWGU - 1 < NFC:
                    wq_slots.append(load_wgu(fc + NWGU - 1))
                if 2 <= fc < 2 + NWD:
                    issue_wd()
                if fc in (2, 4) and pending_final:
                    pending_final.pop(0)()
                    pending_final.pop(0)()
                if fc == 10 and n + 1 == NCH and s + 1 < nseq:
                    nxt_norm = norm1_gen(s + 1, (PS[0], PS[1]))
                    next(nxt_norm)
                if fc in (6, 12) and n + 1 < NCH:
                    for ttl_ in ((0, 1) if fc == 6 else (2, 3)):
                        group_norm_tile(0, 4 * (n + 1) + ttl_, g_foxC)
                        group_norm_tile(1, 4 * (n + 1) + ttl_, g_mlaC)
                bg = PS[(fc % 2) * 2]
                bu = PS[(fc % 2) * 2 + 1]
                for (bk, gi) in ((bg, 0), (bu, 1)):
                    with cx.grp("pe", reads=[slot, h2], writes=[bk]) as hh:
                        for c in range(8):
                            hh[0] = T.matmul(bk.ap, lhsT=slot.ap[:, gi, c, :], rhs=h2.ap[:, c, :],
                                             start=(c == 0), stop=(c == 7))
                sgb = sg[fc % 2]
                cx.op("act", lambda a: a.activation(out=sgb.ap, in_=bg.ap, func=AF.Silu), reads=[bg], writes=[sgb])
                cx.op("dve", lambda v: v.tensor_tensor(out=aT[fc].ap, in0=sgb.ap, in1=bu.ap, op=ALU.mult),
                      reads=[sgb, bu], writes=[aT[fc]])
            if n + 1 < NCH:
                next_wq = [load_wgu(i_) for i_ in range(NWGU - 1)]
            if n + 1 < NCH:
                nxt = stageA(n + 1, 1 - p)
            elif s + 1 < nseq:
                nxt = nxt_norm
            else:
                nxt = None
            for dh in range(2):
                dbank = [PS[4], PS[5], PS[6], PS[7]]
                for f2 in range(NFC // 2):
                    slot = wd_loaded[dh * (NFC // 2) + f2]
                    if f2 in (0, NFC // 2 - 1):
                        for ttl in range(4):
                            with cx.grp("pe", reads=[slot, aT[2 * f2], aT[2 * f2 + 1]], writes=[dbank[ttl]]) as hh:
                                for a_ in range(2):
                                    fc = 2 * f2 + a_
                                    hh[0] = T.matmul(dbank[ttl].ap, lhsT=aT[fc].ap[:, ttl * 128:(ttl + 1) * 128],
                                                     rhs=slot.ap[:, a_, :], start=(fc == 0), stop=(fc == NFC - 1))
                    else:
                        with cx.grp("pe", reads=[slot, aT[2 * f2], aT[2 * f2 + 1]], writes=[]) as hh:
                            for a_ in range(2):
                                fc = 2 * f2 + a_
                                for ttl in range(4):
                                    hh[0] = T.matmul(dbank[ttl].ap, lhsT=aT[fc].ap[:, ttl * 128:(ttl + 1) * 128],
                                                     rhs=slot.ap[:, a_, :], start=(fc == 0), stop=(fc == NFC - 1))
                    issue_wd()
                    if f2 == NFC // 2 - 1:
                        for ttl in range(4):
                            cx.op("dve", lambda v: v.tensor_tensor(out=xx[ttl].ap[:, dh * 512:(dh + 1) * 512],
                                                                   in0=dbank[ttl].ap,
                                                                   in1=xx[ttl].ap[:, dh * 512:(dh + 1) * 512],
                                                                   op=ALU.add),
                                  reads=[dbank[ttl], xx[ttl]], writes=[xx[ttl]])
                    if nxt is not None:
                        try:
                            next(nxt)
                        except StopIteration:
                            nxt = None
            if nxt is not None:
                drain(nxt)
            def final_tile(ttl, n=n, xx=xx):
                tt = 4 * n + ttl
                ob = outt[ttl % 2]
                st = new_stat()
                cx.op("act", lambda a: a.activation(out=ob.ap, in_=xx[ttl].ap, func=AF.Square,
                                                    accum_out=st.ap[:, 0:1]), reads=[xx[ttl]], writes=[ob, st])
                rstd_from(st, 1024)
                cx.op("dve", lambda v: v.scalar_tensor_tensor(out=ob.ap, in0=xx[ttl].ap, scalar=st.ap[:, 1:2],
                                                              in1=g_fin.ap, op0=ALU.mult, op1=ALU.mult),
                      reads=[xx[ttl], st, g_fin], writes=[ob])
                cx.dma("pool", out[s, tt * 128:(tt + 1) * 128, :], ob.ap, reads=[ob], writes=[], semof=ob)

            if n + 1 < NCH:
                pending_final = [(lambda t=t_, f=final_tile: f(t)) for t_ in range(4)]
            else:
                for t_ in range(4):
                    final_tile(t_)

    for ob in outt:
        for k, t in ob.rds.items():
            cx.wait("pool", t)
    print(f"[kernel] SBUF arena: A={arenaBC - base0} B={endB - arenaBC} C={endC - arenaBC} top={top - base0}")
    return nc


def _freq_tab():
    inv = (np.float32(10000.0) ** (-(np.arange(0, 32, 2, dtype=np.float32)) / np.float32(32))).astype(np.float32)
    tab = np.zeros((128, 4), np.float32)
    tab[64:80, 0] = inv
    tab[80:96, 0] = inv
    tab[64:96, 1] = np.float32(math.pi / 2)
    tab[64:80, 2] = -inv
    tab[80:96, 2] = inv
    return tab


def _in_map(xs, ps, inputs):
    f = lambda a: np.ascontiguousarray(np.asarray(a, dtype=np.float32))
    return {
        "x": f(xs), "positions": np.ascontiguousarray(np.asarray(ps, dtype=np.int32)),
        "norm_mix_g": f(inputs["norm_mix_g"]).reshape(1, D),
        "w_in": f(inputs["w_in"]).reshape(D, 1960),
        "b_fgate": f(inputs["b_fgate"]).reshape(8, 1),
        "q_norm_g": f(inputs["q_norm_g"]).reshape(256, 1),
        "w_uq": f(inputs["w_uq"]).reshape(256, 768),
        "kv_norm_g": f(inputs["kv_norm_g"]).reshape(128, 1),
        "w_ukv": f(inputs["w_ukv"]).reshape(128, 1024),
        "fox_out_g": f(inputs["fox_out_g"]).reshape(1, 512),
        "mla_out_g": f(inputs["mla_out_g"]).reshape(1, 512),
        "w_o": f(inputs["w_o"]).reshape(D, D),
        "norm_ffn_g": f(inputs["norm_ffn_g"]).reshape(1, D),
        "w_gate": f(inputs["w_gate"]).reshape(D, DFF),
        "w_up": f(inputs["w_up"]).reshape(D, DFF),
        "w_down": f(inputs["w_down"]).reshape(DFF, D),
        "final_norm_g": f(inputs["final_norm_g"]).reshape(1, D),
        "freq_tab": _freq_tab(),
    }


def kernel(**inputs):
    ncores = 8
    x = np.asarray(inputs["x"])
    p = np.asarray(inputs["positions"])
    nseq = x.shape[0] // ncores
    nc = build_nc(nseq)
    in_maps = [_in_map(x[i * nseq:(i + 1) * nseq], p[i * nseq:(i + 1) * nseq], inputs) for i in range(ncores)]
    res = run_bass_kernel_spmd(nc, in_maps, core_ids=list(range(ncores)))
    return np.concatenate([np.asarray(r["out"]) for r in res.results], axis=0).astype(np.float32)
```

```python
import math
from contextlib import contextmanager
import numpy as np
import concourse.bass as bass
import concourse.mybir as mybir
from concourse.bass_utils import run_bass_kernel_spmd

F32 = mybir.dt.float32
BF16 = mybir.dt.bfloat16
I32 = mybir.dt.int32
AF = mybir.ActivationFunctionType
ALU = mybir.AluOpType

S = 2048
D = 1024
NT = S // 128
NCH = S // 512
DFF = 2816
NFC = DFF // 128
EPS = 1e-6
C_F = 1536
C_QL = 1544
C_KV = 1800
C_KR = 1928
WIN = 1992
TWO_PI = 2.0 * math.pi
CW1 = 6.28125
CW2 = TWO_PI - CW1


class Buf:
    registry = []

    def __init__(self, name, ap, excl=False, parent=None):
        self.name = name
        self.ap = ap
        self.excl = excl
        self.rng = parent.rng if parent is not None else None
        self.arena = parent.arena if parent is not None else None
        self.alias = []
        Buf.registry.append(self)
        self.wr = None
        self.rds = {}
        self.dsem = None
        self.dcount = 0


class Ctx:
    def __init__(self, nc):
        self.nc = nc
        self.eng = {"pe": nc.tensor, "act": nc.scalar, "dve": nc.vector, "pool": nc.gpsimd, "sp": nc.sync}
        self.psem = {k: nc.alloc_semaphore("prog_" + k) for k in ("pe", "act", "dve", "pool")}
        self.pcount = {k: 0 for k in self.psem}
        self.waited = {}
        self.dma_last = {}

    def wait(self, e, tok):
        if tok is None:
            return
        sem, val, key = tok
        if e == "pe" and key == "pe":
            return
        if self.waited.get((e, key), 0) >= val:
            return
        self.eng[e].wait_ge(sem, val)
        self.waited[(e, key)] = val

    def deps(self, e, reads, writes):
        for b in reads:
            self.wait(e, b.wr)
            if b.excl:
                for k, t in b.rds.items():
                    if k != e:
                        self.wait(e, t)
        for b in writes:
            self.wait(e, b.wr)
            for k, t in b.rds.items():
                self.wait(e, t)
            for a in b.alias:
                self.wait(e, a.wr)
                for k, t in a.rds.items():
                    self.wait(e, t)

    def done(self, e, inst, reads, writes):
        self.pcount[e] += 1
        inst.then_inc(self.psem[e], 1)
        tok = (self.psem[e], self.pcount[e], e)
        for b in reads:
            b.rds[e] = tok
        for b in writes:
            b.wr = tok
            b.rds = {}
        return tok

    def op(self, e, fn, reads=(), writes=()):
        self.deps(e, reads, writes)
        inst = fn(self.eng[e])
        return self.done(e, inst, reads, writes)

    @contextmanager
    def grp(self, e, reads=(), writes=()):
        self.deps(e, reads, writes)
        h = [None]
        yield h
        self.done(e, h[0], reads, writes)

    def dma(self, q, out, in_, reads=(), writes=(), semof=None):
        self.deps(q, reads, writes)
        sb = semof if semof is not None else writes[0]
        if sb.dsem is None:
            sb.dsem = self.nc.alloc_semaphore("d_" + sb.name)
        inst = self.eng[q].dma_start(out=out, in_=in_)
        sb.dcount += 16
        inst.then_inc(sb.dsem, 16)
        key = "d_" + sb.name
        tok = (sb.dsem, sb.dcount, key)
        self.dma_last[key] = tok
        for b in reads:
            b.rds[key] = tok
        for b in writes:
            b.wr = tok
            b.rds = {}
        return tok

    def barrier(self, dummy):
        for e in ("pe", "act", "dve"):
            if self.pcount[e]:
                self.wait("pool", (self.psem[e], self.pcount[e], e))
        for tok in self.dma_last.values():
            self.wait("pool", tok)
        tok = self.op("pool", lambda g: g.memset(dummy.ap, 0.0), writes=[dummy])
        for e in ("pe", "act", "dve", "sp"):
            self.wait(e, tok)


def build_nc(nseq):
    nc = bass.Bass("TRN2", target_bir_lowering=False)
    cx = Ctx(nc)
    T, A, V, G = nc.tensor, nc.scalar, nc.vector, nc.gpsimd

    def din(name, shape, dt=F32):
        return nc.dram_tensor(name, list(shape), dt, kind="ExternalInput").ap()

    x = din("x", [nseq, S, D])
    pos = din("positions", [nseq, S], I32)
    g_mix_d = din("norm_mix_g", [1, D])
    w_in_d = din("w_in", [D, 1960])
    bf_d = din("b_fgate", [8, 1])
    qg_d = din("q_norm_g", [256, 1])
    w_uq_d = din("w_uq", [256, 768])
    kvg_d = din("kv_norm_g", [128, 1])
    w_ukv_d = din("w_ukv", [128, 1024])
    g_fox_d = din("fox_out_g", [1, 512])
    g_mla_d = din("mla_out_g", [1, 512])
    w_o_d = din("w_o", [D, D])
    g_ffn_d = din("norm_ffn_g", [1, D])
    w_g_d = din("w_gate", [D, DFF])
    w_u_d = din("w_up", [D, DFF])
    w_d_d = din("w_down", [DFF, D])
    g_fin_d = din("final_norm_g", [1, D])
    freq_d = din("freq_tab", [128, 4])
    out = nc.dram_tensor("out", [nseq, S, D], F32, kind="ExternalOutput").ap()

    def dscr(name, shape):
        return nc.dram_tensor(name, list(shape), BF16, kind="Internal").ap()

    win_b = dscr("win_b", [D, WIN])
    wo_b = dscr("wo_b", [D, D])
    wgu_b = dscr("wgu_b", [NFC, 128, 2, 8, 128])
    wd_b = dscr("wd_b", [DFF, D])

    base0 = (nc.sbuf_base + 63) // 64 * 64
    top = nc.sbuf_top
    cur = [base0]

    Buf.registry = []
    arena = ["A"]

    def sb(name, shape, dt, excl=False):
        nbytes = int(np.prod(shape[1:])) * (4 if dt in (F32, I32) else 2)
        off = cur[0]
        cur[0] = (off + nbytes + 63) // 64 * 64
        assert cur[0] <= top, f"SBUF overflow at {name}: {cur[0]} > {top}"
        t = nc.alloc_sbuf_tensor_at(name, list(shape), dt, offset=off)
        b = Buf(name, t.ap())
        b.rng = (off, off + nbytes)
        b.arena = arena[0]
        return b

    O_all = sb("O_all", [128, NT, 1024], BF16)
    ident = sb("ident", [128, 128], BF16)
    maskT = sb("maskT", [128, 128], BF16)
    ones_bf = sb("ones_bf", [128, 128], BF16)
    wuqm = sb("wuqm", [128, 2, 8, 128], BF16)
    wukv = sb("wukv", [128, 1024], BF16)
    freq = sb("freq", [128, 4], F32)
    negb = sb("negb", [128, 1], F32)
    qg = sb("qg", [128, 2], F32)
    kvg = sb("kvg", [128, 1], F32)
    ones8 = sb("ones8", [128, 1], F32)
    dummy = sb("dummy", [128, 1], F32)
    stats = [sb(f"st{i}", [128, 2], F32) for i in range(8)]
    rrs = [sb(f"rr{i}", [128, 4], F32) for i in range(2)]
    rkv_tok = sb("rkv_tok", [128, NT], F32)
    sqj = [sb(f"sqj{i}", [128, 512], BF16) for i in range(2)]
    arenaBC = cur[0]

    arena[0] = "B"
    hT = sb("hT", [128, 8, S], BF16)
    hTc = [Buf(f"hTc{i}", hT.ap, parent=hT) for i in range(NCH)]
    xin = [sb(f"xin{i}", [128, 1024], F32) for i in range(3)]
    hbf = [sb(f"hbf{i}", [128, 1024], BF16) for i in range(2)]
    g_mix = sb("g_mix", [128, 1024], F32)
    w_in = sb("w_in", [128, 8, WIN], BF16)
    ropeS = sb("ropeS", [128, 1024], F32)
    w_qk = Buf("w_qk", w_in.ap, parent=w_in)
    w_vf = Buf("w_vf", w_in.ap, parent=w_in)
    w_lat = Buf("w_lat", w_in.ap, parent=w_in)
    w_kr = Buf("w_kr", w_in.ap, parent=w_in)
    TMP = sb("TMP", [128, S], F32)
    T1 = sb("T1", [128, S], F32)
    T2 = sb("T2", [128, S], F32)
    q_latT = sb("q_latT", [128, 2, S], BF16)
    kv_latT = sb("kv_latT", [128, S], BF16)
    k_peT = sb("k_peT", [128, S], BF16)
    QT = [sb(f"QT{i}", [128, S], BF16) for i in range(2)]
    KT = [sb(f"KT{i}", [128, S], BF16) for i in range(2)]
    QTa = [Buf(f"QTa{i}", QT[i].ap, parent=QT[i]) for i in range(2)]
    KTa = [Buf(f"KTa{i}", KT[i].ap, parent=KT[i]) for i in range(2)]
    V_all = sb("V_all", [128, NT, 8, 66], BF16)
    PT = [sb(f"PT{i}", [128, 512], BF16) for i in range(4)]
    sq = [Buf(f"sq{i}", hbf[i].ap[:, 0:512], parent=hbf[i]) for i in range(2)]
    T1h = Buf("T1h", T1.ap, parent=T1)
    T2h = Buf("T2h", T2.ap, parent=T2)
    TMPh = Buf("TMPh", TMP.ap, parent=TMP)
    endB = cur[0]

    cur[0] = arenaBC
    arena[0] = "C"
    w_o = sb("w_o", [128, 8, 1024], BF16)
    g_ffn = sb("g_ffn", [128, 1024], F32)
    OT0 = sb("OT0", [128, 8, 512], BF16)
    xr = [sb(f"xr{i}", [128, 1024], F32) for i in range(2)]
    x1_0 = [sb(f"x1_0_{i}", [128, 1024], F32) for i in range(4)]
    hbf2 = [sb(f"hbf2_{i}", [128, 1024], BF16) for i in range(2)]
    h2T0 = sb("h2T0", [128, 8, 512], BF16)
    g_foxC = sb("g_foxC", [128, 512], F32)
    g_mlaC = sb("g_mlaC", [128, 512], F32)
    NWGU, NWD = 4, 6
    wgu = [sb(f"wgu{i}", [128, 2, 8, 128], BF16) for i in range(NWGU)]
    g_fin = sb("g_fin", [128, 1024], F32)
    OT = [OT0, sb("OT1", [128, 8, 512], BF16)]
    h2T = [h2T0, sb("h2T1", [128, 8, 512], BF16)]
    x1 = [x1_0, [sb(f"x1_1_{i}", [128, 1024], F32) for i in range(4)]]
    aT = [sb(f"aT{i}", [128, 512], BF16) for i in range(NFC)]
    wd = [sb(f"wd{i}", [128, 2, 512], BF16) for i in range(NWD)]
    sg = [sb(f"sg{i}", [128, 512], BF16) for i in range(2)]
    outt = [sb(f"outt{i}", [128, 1024], F32) for i in range(2)]
    endC = cur[0]

    PS = [Buf(f"ps{i}", nc.alloc_psum_tensor(f"ps{i}", [128, 512], F32).ap(), excl=True) for i in range(8)]

    for b1 in Buf.registry:
        if b1.arena == "B":
            for b2 in Buf.registry:
                if b2.arena == "C" and b1.rng[0] < b2.rng[1] and b2.rng[0] < b1.rng[1]:
                    b1.alias.append(b2)
                    b2.alias.append(b1)

    win_bB = Buf("win_bB", win_b)
    win_qkB = Buf("win_qkB", win_b)
    wo_bB = Buf("wo_bB", wo_b)
    wgu_bB = Buf("wgu_bB", wgu_b)
    wd_bB = Buf("wd_bB", wd_b)

    stat_i = [0]

    def new_stat():
        stat_i[0] += 1
        return stats[stat_i[0] % len(stats)]

    cx.op("pool", lambda g: g.memset(ident.ap, 0.0), writes=[ident])
    cx.op("pool", lambda g: g.affine_select(out=ident.ap, in_=ident.ap, pattern=[[-1, 128]],
                                            compare_op=ALU.not_equal, fill=1.0, base=0,
                                            channel_multiplier=1), reads=[ident], writes=[ident])
    cx.op("pool", lambda g: g.memset(maskT.ap, 0.0), writes=[maskT])
    cx.op("pool", lambda g: g.affine_select(out=maskT.ap, in_=maskT.ap, pattern=[[1, 128]],
                                            compare_op=ALU.is_ge, fill=-30000.0, base=0,
                                            channel_multiplier=-1), reads=[maskT], writes=[maskT])
    cx.op("pool", lambda g: g.memset(ones_bf.ap, 1.0), writes=[ones_bf])
    cx.op("pool", lambda g: g.memset(ones8.ap, 1.0), writes=[ones8])
    cx.dma("sp", freq.ap, freq_d, writes=[freq])
    cx.dma("sp", negb.ap[0:8, :], bf_d, writes=[negb])
    for c2 in range(2):
        cx.dma("sp", qg.ap[:, c2:c2 + 1], qg_d[c2 * 128:(c2 + 1) * 128, :], writes=[qg])
    cx.dma("sp", kvg.ap, kvg_d, writes=[kvg])
    cx.op("dve", lambda v: v.tensor_scalar(out=negb.ap[0:8, :], in0=negb.ap[0:8, :], scalar1=-1.0,
                                           scalar2=None, op0=ALU.mult), reads=[negb], writes=[negb])
    cx.dma("pool", win_b[:, 1024:1960], w_in_d[:, 1024:1960], writes=[win_bB])
    cx.dma("pool", win_b[:, 1960:1976], w_in_d[:, C_KR + 16:C_KR + 32], writes=[win_bB])
    cx.dma("pool", win_b[:, 1976:1992], w_in_d[:, C_KR:C_KR + 16], writes=[win_bB])
    win_v0 = win_b.rearrange("(c p) e -> p c e", p=128)
    w0sem = {k: Buf("w0sem_" + k, w_in.ap) for k in ("w_vf", "w_kr", "w_lat", "w_qk")}
    cx.dma("pool", w_in.ap[:, :, 1024:1544], win_v0[:, :, 1024:1544], reads=[win_bB], writes=[w_vf], semof=w0sem["w_vf"])
    cx.dma("pool", w_in.ap[:, :, 1864:WIN], win_v0[:, :, 1864:WIN], reads=[win_bB], writes=[w_kr], semof=w0sem["w_kr"])
    cx.dma("pool", w_in.ap[:, :, 1544:1864], win_v0[:, :, 1544:1864], reads=[win_bB], writes=[w_lat], semof=w0sem["w_lat"])
    for rb in range(8):
        rows = slice(rb * 128, (rb + 1) * 128)
        for j in range(2):
            cx.dma("pool", win_b[rows, 0:1024].rearrange("d (h j e) -> d h j e", h=8, j=2)[:, :, j, :],
                   w_in_d[rows, j * 512:(j + 1) * 512].rearrange("d (h e) -> d h e", h=8), writes=[win_qkB])
    cx.dma("pool", w_in.ap[:, :, 0:1024], win_v0[:, :, 0:1024], reads=[win_qkB], writes=[w_qk], semof=w0sem["w_qk"])
    wuq_v = w_uq_d.rearrange("(c p) (h e) -> p c h e", p=128, h=8)
    for c2 in range(2):
        cx.dma("pool", wuqm.ap[:, c2, :, 0:96], wuq_v[:, c2, :, 0:96], writes=[wuqm])
        cx.dma("pool", wuqm.ap[:, c2, :, 96:112], wuq_v[:, c2, :, 80:96], writes=[wuqm])
        cx.dma("pool", wuqm.ap[:, c2, :, 112:128], wuq_v[:, c2, :, 64:80], writes=[wuqm])
    cx.dma("pool", wukv.ap, w_ukv_d, writes=[wukv])
    deferred = []
    deferred.append(lambda: cx.dma("pool", wo_b, w_o_d, writes=[wo_bB]))
    wg_v = w_g_d.rearrange("(c p) (fc f) -> fc p c f", p=128, f=128)
    wu_v = w_u_d.rearrange("(c p) (fc f) -> fc p c f", p=128, f=128)
    for fc in range(NFC):
        deferred.append(lambda fc=fc: cx.dma("pool", wgu_b[fc, :, 0, :, :], wg_v[fc], writes=[wgu_bB]))
        deferred.append(lambda fc=fc: cx.dma("pool", wgu_b[fc, :, 1, :, :], wu_v[fc], writes=[wgu_bB]))
    for r in range(0, DFF, 704):
        deferred.append(lambda r=r: cx.dma("pool", wd_b[r:r + 704, :], w_d_d[r:r + 704, :], writes=[wd_bB]))

    def run_deferred(k):
        for _ in range(k):
            if deferred:
                deferred.pop(0)()

    def rstd_from(st, n):
        cx.op("act", lambda a: a.activation(out=st.ap[:, 1:2], in_=st.ap[:, 0:1], func=AF.Ln,
                                            scale=1.0 / n, bias=EPS), reads=[st], writes=[st])
        cx.op("act", lambda a: a.activation(out=st.ap[:, 1:2], in_=st.ap[:, 1:2], func=AF.Exp,
                                            scale=-0.5), reads=[st], writes=[st])

    def rms_to_bf(xb, xap, gb, hb):
        st = new_stat()
        cx.op("act", lambda a: a.activation(out=hb.ap, in_=xap, func=AF.Square,
                                            accum_out=st.ap[:, 0:1]), reads=[xb], writes=[hb, st])
        rstd_from(st, 1024)
        cx.op("dve", lambda v: v.scalar_tensor_tensor(out=hb.ap, in0=xap, scalar=st.ap[:, 1:2],
                                                      in1=gb.ap, op0=ALU.mult, op1=ALU.mult),
              reads=[xb, st, gb], writes=[hb])

    def transpose_pe(src_b, src_ap, bank):
        psb = bank.ap.bitcast(BF16)
        with cx.grp("pe", reads=[src_b, ident], writes=[bank]) as h:
            for c in range(8):
                h[0] = T.transpose(out=psb[:, c * 128:(c + 1) * 128], in_=src_ap[:, c * 128:(c + 1) * 128],
                                   identity=ident.ap)

    def transpose_evac(bank, dst_b, dst_ap):
        psb = bank.ap.bitcast(BF16)
        cx.op("dve", lambda v: v.tensor_copy(out=dst_ap, in_=psb.rearrange("p (c t) -> p c t", c=8)),
              reads=[bank], writes=[dst_b])

    def transpose_to(src_b, src_ap, bank, dst_b, dst_ap, evac="dve"):
        psb = bank.ap.bitcast(BF16)
        with cx.grp("pe", reads=[src_b, ident], writes=[bank]) as h:
            for c in range(8):
                h[0] = T.transpose(out=psb[:, c * 128:(c + 1) * 128], in_=src_ap[:, c * 128:(c + 1) * 128],
                                   identity=ident.ap)
        if evac == "act":
            cx.op("act", lambda a: a.copy(out=dst_ap, in_=psb.rearrange("p (c t) -> p c t", c=8)),
                  reads=[bank], writes=[dst_b])
        else:
            cx.op("dve", lambda v: v.tensor_copy(out=dst_ap, in_=psb.rearrange("p (c t) -> p c t", c=8)),
                  reads=[bank], writes=[dst_b])

    def attention(QTb, QTab, KTb, KTab, dk, hv, scale, ocol, filler, filler_start=0, filler_rate=2):
        qt, kt_ = QTb.ap, KTb.ap
        steps = []
        for j in range(NCH):
            for kt in range(4 * j + 4):
                steps.append((j, kt))
        nst = len(steps)
        sbank = [PS[0], PS[1], PS[2]]
        obank = [PS[3], PS[4]]

        def qk(n):
            j, kt = steps[n]
            i = kt - 4 * j
            q0 = max(0, i) * 128
            bk = sbank[n % 3]
            with cx.grp("pe", reads=[QTb, QTab, KTb, KTab, ident, maskT], writes=[bk]) as h:
                h[0] = T.matmul(bk.ap[:, q0:512], lhsT=kt_[0:dk, kt * 128:(kt + 1) * 128],
                                rhs=qt[0:dk, j * 512 + q0:(j + 1) * 512], start=True, stop=(i < 0))
                if i >= 0:
                    h[0] = T.matmul(bk.ap[:, q0:q0 + 128], lhsT=ident.ap, rhs=maskT.ap, start=False, stop=True)

        def ex(n):
            j, kt = steps[n]
            q0 = max(0, kt - 4 * j) * 128
            bk = sbank[n % 3]
            pt = PT[n % 4]
            cx.op("act", lambda a: a.activation(out=pt.ap[:, q0:512], in_=bk.ap[:, q0:512], func=AF.Exp,
                                                scale=scale), reads=[bk], writes=[pt])

        def pv(n):
            j, kt = steps[n]
            i = kt - 4 * j
            pt = PT[n % 4]
            ob = obank[j % 2]
            ov = ob.ap[:, 0:260].rearrange("p (a b) -> p a b", a=4)
            with cx.grp("pe", reads=[pt, V_all], writes=[ob]) as h:
                for qb in range(max(0, i), 4):
                    h[0] = T.matmul(ov[:, qb, :], lhsT=pt.ap[:, qb * 128:(qb + 1) * 128],
                                    rhs=V_all.ap[:, kt, hv, 0:65], start=(kt == 0 and qb == 0),
                                    stop=(kt == 4 * j + qb), skip_group_check=True)
            if kt == 4 * j + 3:
                rr = rrs[j % 2]
                cx.op("dve", lambda v: v.reciprocal(out=rr.ap, in_=ov[:, :, 64]), reads=[ob], writes=[rr])
                cx.op("dve", lambda v: v.tensor_tensor(
                    out=O_all.ap[:, 4 * j:4 * j + 4, ocol:ocol + 64], in0=ov[:, :, 0:64],
                    in1=rr.ap.unsqueeze(2).broadcast_to([128, 4, 64]), op=ALU.mult),
                    reads=[ob, rr], writes=[O_all])

        qk(0)
        qk(1)
        for n in range(nst):
            ex(n)
            if n + 2 < nst:
                qk(n + 2)
            pv(n)
            if filler is not None and n >= filler_start:
                for _ in range(filler_rate):
                    try:
                        next(filler)
                    except StopIteration:
                        filler = None
                        break
        if filler is not None:
            for _ in filler:
                pass

    pbank = [PS[5], PS[6]]
    fbank = [PS[5], PS[6], PS[7]]
    pbi = [0]

    def next_pbank():
        pbi[0] += 1
        return pbank[pbi[0] % 2]

    def next_fbank():
        pbi[0] += 1
        return fbank[pbi[0] % 3]

    def fox_proj(h, b):
        cst = T2.ap.bitcast(BF16).rearrange("p (a s) -> p a s", a=2)
        cx.op("pool", lambda g: g.memset(QT[b].ap[64:68, :], -1.0), writes=[QTa[b]])
        cx.op("pool", lambda g: g.memset(KT[b].ap[64:68, :], 1.0), writes=[KTa[b]])
        for r_ in range(2):
            cx.dma("pool", QT[b].ap[64 + r_:65 + r_, :], cst[h:h + 1, r_, :], reads=[T2], writes=[QTa[b]])
            cx.dma("pool", KT[b].ap[66 + r_:67 + r_, :], cst[h:h + 1, r_, :], reads=[T2], writes=[KTa[b]])
        yield
        for n in range(NCH):
            bk = next_fbank()
            with cx.grp("pe", reads=[w_qk] + hTc, writes=[bk]) as hh:
                for c in range(8):
                    hh[0] = T.matmul(bk.ap, lhsT=w_in.ap[:, c, h * 128:(h + 1) * 128],
                                     rhs=hT.ap[:, c, n * 512:(n + 1) * 512], start=(c == 0), stop=(c == 7))
                    if c == 3:
                        yield
            cx.op("dve", lambda v: v.tensor_copy(out=QT[b].ap[0:64, n * 512:(n + 1) * 512], in_=bk.ap[0:64, :]),
                  reads=[bk], writes=[QT[b]])
            cx.op("dve", lambda v: v.tensor_copy(out=KT[b].ap[0:64, n * 512:(n + 1) * 512], in_=bk.ap[64:128, :]),
                  reads=[bk], writes=[KT[b]])
            yield

    rope_i = [0]

    def mla_proj(h, b):
        cx.op("pool", lambda g: g.tensor_copy(out=KT[b].ap[64:96, :], in_=k_peT.ap[64:96, :]),
              reads=[k_peT], writes=[KTa[b]])
        yield
        if 3 <= h <= 6:
            group_norm_tile(0, h - 3, g_foxC)
            yield
        for n in range(NCH):
            cs = slice(n * 512, (n + 1) * 512)
            rb = ropeS
            rA, rB = rb.ap[64:96, 0:512], rb.ap[64:96, 512:1024]
            bk = next_fbank()
            with cx.grp("pe", reads=[wukv, kv_latT], writes=[bk]) as hh:
                hh[0] = T.matmul(bk.ap[0:64, :], lhsT=wukv.ap[:, h * 128:h * 128 + 64],
                                 rhs=kv_latT.ap[:, cs], start=True, stop=True)
            cx.op("dve", lambda v: v.tensor_tensor(out=KT[b].ap[0:64, cs], in0=bk.ap[0:64, :],
                                                   in1=T2.ap[0:64, cs], op=ALU.mult),
                  reads=[bk, T2], writes=[KT[b]])
            yield
            bq = next_fbank()
            with cx.grp("pe", reads=[wuqm, q_latT], writes=[bq]) as hh:
                for c2 in range(2):
                    hh[0] = T.matmul(bq.ap, lhsT=wuqm.ap[:, c2, h, :],
                                     rhs=q_latT.ap[:, c2, cs], start=(c2 == 0), stop=(c2 == 1))
            cx.op("dve", lambda v: v.tensor_tensor(out=QT[b].ap[0:64, cs], in0=bq.ap[0:64, :],
                                                   in1=TMP.ap[0:64, cs], op=ALU.mult),
                  reads=[bq, TMP], writes=[QT[b]])
            yield
            cx.op("dve", lambda v: v.tensor_tensor(out=rA, in0=bq.ap[64:96, :],
                                                   in1=T1.ap[64:96, cs], op=ALU.mult),
                  reads=[bq, T1h], writes=[rb])
            cx.op("dve", lambda v: v.tensor_tensor(out=rB, in0=bq.ap[96:128, :],
                                                   in1=T2.ap[64:96, cs], op=ALU.mult),
                  reads=[bq, T2h, rb], writes=[rb])
            cx.op("pool", lambda g: g.tensor_tensor(out=QT[b].ap[64:96, cs], in0=rA, in1=rB, op=ALU.add),
                  reads=[rb], writes=[QTa[b]])
            yield

    def drain(gen):
        for _ in gen:
            pass

    def group_norm_tile(gi, tt, gb):
        oap = O_all.ap[:, tt, gi * 512:(gi + 1) * 512]
        st = new_stat()
        jb = sqj[tt % 2]
        cx.op("act", lambda a: a.activation(out=jb.ap, in_=oap, func=AF.Square, accum_out=st.ap[:, 0:1]),
              reads=[O_all], writes=[jb, st])
        rstd_from(st, 512)
        cx.op("dve", lambda v: v.scalar_tensor_tensor(out=oap, in0=oap, scalar=st.ap[:, 1:2], in1=gb.ap,
                                                      op0=ALU.mult, op1=ALU.mult),
              reads=[O_all, st, gb], writes=[O_all])

    def group_norm(gi, gb):
        for tt in range(NT):
            group_norm_tile(gi, tt, gb)

    win_v = win_b.rearrange("(c p) e -> p c e", p=128)

    def load_w(buf, c0, c1):
        src = win_qkB if c0 == 0 else win_bB
        cx.dma("sp", w_in.ap[:, :, c0:c1], win_v[:, :, c0:c1], reads=[src], writes=[buf])

    def norm1_gen(sq_, tbanks):
        cx.dma("sp", g_mix.ap, g_mix_d.broadcast_to([128, D]), writes=[g_mix])

        def xload(t):
            cx.dma("sp", xin[t % 3].ap, x[sq_, t * 128:(t + 1) * 128, :], writes=[xin[t % 3]])

        xload(0)
        xload(1)
        yield
        for tt in range(NT + 2):
            if tt + 2 < NT:
                xload(tt + 2)
            if tt < NT:
                rms_to_bf(xin[tt % 3], xin[tt % 3].ap, g_mix, hbf[tt % 2])
            if 1 <= tt <= NT:
                t = tt - 1
                transpose_pe(hbf[t % 2], hbf[t % 2].ap, tbanks[t % 2])
            if 2 <= tt <= NT + 1:
                t = tt - 2
                transpose_evac(tbanks[t % 2], hTc[t // 4], hT.ap[:, :, t * 128:(t + 1) * 128])
            if sq_ > 0:
                if tt == 0:
                    load_w(w_vf, 1024, 1544)
                if tt == 2:
                    load_w(w_kr, 1864, WIN)
                if tt == 6:
                    load_w(w_qk, 0, 1024)
                if tt == 10:
                    load_w(w_lat, 1544, 1864)
            yield

    wgu_i = [0]
    wd_i = [0]
    for s in range(nseq):
        T1i = T1.ap.bitcast(I32)

        def fox_f_chunk(n):
            bk = next_pbank()
            with cx.grp("pe", reads=[w_vf, hTc[n]], writes=[bk]) as hh:
                for c in range(8):
                    hh[0] = T.matmul(bk.ap[0:8, :], lhsT=w_in.ap[:, c, C_F:C_F + 8],
                                     rhs=hT.ap[:, c, n * 512:(n + 1) * 512], start=(c == 0), stop=(c == 7))
            cx.op("act", lambda a: a.activation(out=TMP.ap[0:8, n * 512:(n + 1) * 512], in_=bk.ap[0:8, :],
                                                func=AF.Exp, scale=-1.0, bias=negb.ap[0:8, 0:1]),
                  reads=[bk, negb], writes=[TMP])

        def fox_v_tile(tt):
            bk = next_pbank()
            with cx.grp("pe", reads=[w_vf, hTc[tt // 4]], writes=[bk]) as hh:
                for c in range(8):
                    hh[0] = T.matmul(bk.ap, lhsT=hT.ap[:, c, tt * 128:(tt + 1) * 128],
                                     rhs=w_in.ap[:, c, 1024:1536], start=(c == 0), stop=(c == 7))
            cx.op("dve", lambda v: v.tensor_copy(out=V_all.ap[:, tt, :, 0:64],
                                                 in_=bk.ap.rearrange("p (h e) -> p h e", h=8)),
                  reads=[bk], writes=[V_all])

        def build_rope_tables():
            cx.op("dve", lambda v: v.tensor_copy(out=TMP.ap[64:96, :], in_=T1i[64:96, :]), reads=[T1h], writes=[TMPh])
            yield
            for (tab, fcol) in ((T1h, 0), (T2h, 2)):
                cx.op("dve", lambda v: v.tensor_scalar(out=tab.ap[64:96, :], in0=TMP.ap[64:96, :],
                                                       scalar1=freq.ap[64:96, fcol:fcol + 1],
                                                       scalar2=freq.ap[64:96, fcol + 1:fcol + 2],
                                                       op0=ALU.mult, op1=ALU.add),
                      reads=[TMPh, freq], writes=[tab])
                yield
                for half in range(2):
                    hs = slice(half * 1024, (half + 1) * 1024)
                    ni = ropeS.ap.bitcast(I32)
                    cx.op("dve", lambda v: v.tensor_scalar(out=ni[64:96, :], in0=tab.ap[64:96, hs],
                                                           scalar1=1.0 / TWO_PI, scalar2=None, op0=ALU.mult),
                          reads=[tab], writes=[ropeS])
                    yield
                    for cw in (CW1, CW2):
                        cx.op("dve", lambda v: v.scalar_tensor_tensor(out=tab.ap[64:96, hs], in0=ni[64:96, :],
                                                                      scalar=-cw, in1=tab.ap[64:96, hs],
                                                                      op0=ALU.mult, op1=ALU.add),
                              reads=[ropeS, tab], writes=[tab])
                        yield
                cx.op("dve", lambda v: v.tensor_scalar(out=tab.ap[64:96, :], in0=tab.ap[64:96, :], scalar1=3.1415925,
                                                       scalar2=-3.1415925, op0=ALU.min, op1=ALU.max),
                      reads=[tab], writes=[tab])
                yield
                cx.op("act", lambda a: a.activation(out=tab.ap[64:96, :], in_=tab.ap[64:96, :], func=AF.Sin),
                      reads=[tab], writes=[tab])

        def kpe_chunk(n):
            cs = slice(n * 512, (n + 1) * 512)
            rA, rB = ropeS.ap[64:96, 0:512], ropeS.ap[64:96, 512:1024]
            ba = next_pbank()
            with cx.grp("pe", reads=[w_kr, hTc[n]], writes=[ba]) as hh:
                for c in range(8):
                    hh[0] = T.matmul(ba.ap, lhsT=w_in.ap[:, c, C_KR - 64:C_KR + 64],
                                     rhs=hT.ap[:, c, cs], start=(c == 0), stop=(c == 7))
            cx.op("dve", lambda v: v.tensor_tensor(out=rA, in0=ba.ap[64:96, :], in1=T1.ap[64:96, cs], op=ALU.mult),
                  reads=[ba, T1h], writes=[ropeS])
            cx.op("dve", lambda v: v.tensor_tensor(out=rB, in0=ba.ap[96:128, :], in1=T2.ap[64:96, cs], op=ALU.mult),
                  reads=[ba, T2h, ropeS], writes=[ropeS])
            cx.op("pool", lambda g: g.tensor_tensor(out=k_peT.ap[64:96, cs], in0=rA, in1=rB, op=ALU.add),
                  reads=[ropeS], writes=[k_peT])

        if s == 0:
            drain(norm1_gen(0, (PS[3], PS[4])))
        T1i = T1.ap.bitcast(I32)
        cx.dma("sp", T1i[64:96, :], pos[s:s + 1, :].broadcast_to([32, S]), writes=[T1h])
        rt = build_rope_tables()
        for n in range(NCH):
            fox_f_chunk(n)
            for t4 in range(4):
                fox_v_tile(4 * n + t4)
                for _ in range(2):
                    next(rt, None)
        drain(rt)
        for n in range(NCH):
            kpe_chunk(n)
        cx.op("pool", lambda g: g.memset(V_all.ap[:, :, :, 64:65], 1.0), writes=[V_all])
        cx.op("act", lambda a: a.activation(out=TMP.ap[0:8, :], in_=TMP.ap[0:8, :], func=AF.Ln, bias=1.0),
              reads=[TMP], writes=[TMP])
        cx.op("dve", lambda v: v.tensor_tensor_scan(out=T1.ap[0:8, :], data0=ones8.ap[0:8, 0:1].broadcast_to([8, S]),
                                                    data1=TMP.ap[0:8, :], initial=0.0, op0=ALU.mult, op1=ALU.add),
              reads=[TMP, ones8], writes=[T1])
        cst = T2.ap.bitcast(BF16).rearrange("p (a s) -> p a s", a=2)
        cx.op("dve", lambda v: v.tensor_scalar(out=cst[0:8, 0, :], in0=T1.ap[0:8, :], scalar1=-8.0, scalar2=None,
                                               op0=ALU.mult), reads=[T1], writes=[T2])
        cx.op("dve", lambda v: v.scalar_tensor_tensor(out=cst[0:8, 1, :], in0=T1.ap[0:8, :], scalar=-8.0,
                                                      in1=cst[0:8, 0, :], op0=ALU.mult, op1=ALU.subtract),
              reads=[T1, T2], writes=[T2])

        drain(fox_proj(0, 0))
        for h in range(8):
            filler = fox_proj(h + 1, (h + 1) % 2) if h < 7 else None
            attention(QT[h % 2], QTa[h % 2], KT[h % 2], KTa[h % 2], 68, h, 0.125, h * 64, filler)
            run_deferred(5)

        for n in range(NCH):
            cs = slice(n * 512, (n + 1) * 512)
            ssq = PS[7]
            for c2 in range(2):
                bk = next_pbank()
                with cx.grp("pe", reads=[w_lat] + hTc, writes=[bk]) as hh:
                    for c in range(8):
                        hh[0] = T.matmul(bk.ap, lhsT=w_in.ap[:, c, C_QL + c2 * 128:C_QL + (c2 + 1) * 128],
                                         rhs=hT.ap[:, c, cs], start=(c == 0), stop=(c == 7))
                sqb = sq[c2]
                cx.op("act", lambda a: a.activation(out=sqb.ap, in_=bk.ap, func=AF.Square), reads=[bk], writes=[sqb])
                cx.op("dve", lambda v: v.tensor_scalar(out=q_latT.ap[:, c2, cs], in0=bk.ap, scalar1=qg.ap[:, c2:c2 + 1],
                                                       scalar2=None, op0=ALU.mult), reads=[bk, qg], writes=[q_latT])
                with cx.grp("pe", reads=[sqb, ones_bf], writes=[ssq]) as hh:
                    hh[0] = T.matmul(ssq.ap, lhsT=ones_bf.ap, rhs=sqb.ap, start=(c2 == 0), stop=(c2 == 1))
            cx.op("act", lambda a: a.activation(out=TMP.ap[0:96, cs], in_=ssq.ap[0:96, :], func=AF.Ln,
                                                scale=1.0 / 256, bias=EPS), reads=[ssq], writes=[TMP, TMPh])
            cx.op("act", lambda a: a.activation(out=TMP.ap[0:96, cs], in_=TMP.ap[0:96, cs], func=AF.Exp, scale=-0.5),
                  reads=[TMP, TMPh], writes=[TMP, TMPh])
        for tab in (T1h, T2h):
            cx.op("dve", lambda v: v.tensor_tensor(out=tab.ap[64:96, :], in0=tab.ap[64:96, :], in1=TMP.ap[64:96, :],
                                                   op=ALU.mult), reads=[tab, TMPh, k_peT], writes=[tab])
        for n in range(NCH):
            cs = slice(n * 512, (n + 1) * 512)
            ssq = PS[7]
            bk = next_pbank()
            with cx.grp("pe", reads=[w_lat, w_kr] + hTc, writes=[bk]) as hh:
                for c in range(8):
                    hh[0] = T.matmul(bk.ap, lhsT=w_in.ap[:, c, C_KV:C_KV + 128], rhs=hT.ap[:, c, cs],
                                     start=(c == 0), stop=(c == 7))
            sqb = sq[n % 2]
            cx.op("act", lambda a: a.activation(out=sqb.ap, in_=bk.ap, func=AF.Square), reads=[bk], writes=[sqb])
            cx.op("dve", lambda v: v.tensor_scalar(out=kv_latT.ap[:, cs], in0=bk.ap, scalar1=kvg.ap[:, 0:1],
                                                   scalar2=None, op0=ALU.mult), reads=[bk, kvg], writes=[kv_latT])
            with cx.grp("pe", reads=[sqb, ones_bf], writes=[ssq]) as hh:
                hh[0] = T.matmul(ssq.ap, lhsT=ones_bf.ap, rhs=sqb.ap, start=True, stop=True)
            cx.op("act", lambda a: a.activation(out=T2.ap[0:64, cs], in_=ssq.ap[0:64, :], func=AF.Ln,
                                                scale=1.0 / 128, bias=EPS), reads=[ssq], writes=[T2])
            cx.op("act", lambda a: a.activation(out=T2.ap[0:64, cs], in_=T2.ap[0:64, cs], func=AF.Exp, scale=-0.5),
                  reads=[T2], writes=[T2])
            tb = next_pbank()
            with cx.grp("pe", reads=[sqb, ones_bf], writes=[tb]) as hh:
                for ttl in range(4):
                    hh[0] = T.matmul(tb.ap[:, ttl:ttl + 1], lhsT=sqb.ap[:, ttl * 128:(ttl + 1) * 128],
                                     rhs=ones_bf.ap[:, 0:1], start=True, stop=True, skip_group_check=True)
            cx.op("act", lambda a: a.activation(out=rkv_tok.ap[:, 4 * n:4 * n + 4], in_=tb.ap[:, 0:4], func=AF.Ln,
                                                scale=1.0 / 128, bias=EPS), reads=[tb], writes=[rkv_tok])
        cx.op("act", lambda a: a.activation(out=rkv_tok.ap, in_=rkv_tok.ap, func=AF.Exp, scale=-0.5),
              reads=[rkv_tok], writes=[rkv_tok])
        wv = wukv.ap.rearrange("p (h e) -> p h e", h=8)[:, :, 64:128]
        for tt in range(NT):
            bk = next_pbank()
            with cx.grp("pe", reads=[wukv, kv_latT], writes=[bk]) as hh:
                hh[0] = T.matmul(bk.ap.rearrange("p (h e) -> p h e", h=8), lhsT=kv_latT.ap[:, tt * 128:(tt + 1) * 128],
                                 rhs=wv, start=True, stop=True)
            cx.op("dve", lambda v: v.tensor_scalar(out=V_all.ap[:, tt, :, 0:64],
                                                   in0=bk.ap.rearrange("p (h e) -> p h e", h=8),
                                                   scalar1=rkv_tok.ap[:, tt:tt + 1], scalar2=None, op0=ALU.mult),
                  reads=[bk, rkv_tok], writes=[V_all])
        def load_wgu(fc):
            slot = wgu[wgu_i[0] % NWGU]
            wgu_i[0] += 1
            cx.dma("sp", slot.ap, wgu_b[fc], reads=[wgu_bB], writes=[slot])
            return slot

        def load_wd(dh, f2):
            slot = wd[wd_i[0] % NWD]
            wd_i[0] += 1
            cx.dma("sp", slot.ap, wd_b[f2 * 256:(f2 + 1) * 256, dh * 512:(dh + 1) * 512].rearrange(
                "(a p) e -> p a e", p=128), reads=[wd_bB], writes=[slot])
            return slot

        def stageA(n, p, obk=(PS[0], PS[1]), tbk=(PS[2], PS[3]), do_gn=False):
            ot, h2, xx = OT[p], h2T[p], x1[p]

            def gn(ttl):
                group_norm_tile(1, 4 * n + ttl, g_mlaC)

            def trO(ttl):
                transpose_to(O_all, O_all.ap[:, 4 * n + ttl, :], tbk[ttl % 2], ot, ot.ap[:, :, ttl * 128:(ttl + 1) * 128])

            def xload(ttl):
                tt = 4 * n + ttl
                cx.dma("sp", xr[ttl % 2].ap, x[s, tt * 128:(tt + 1) * 128, :], writes=[xr[ttl % 2]])

            def oproj(ttl, dh):
                bk = obk[dh]
                xb = xr[ttl % 2]
                with cx.grp("pe", reads=[ot, w_o], writes=[bk]) as hh:
                    for c in range(8):
                        hh[0] = T.matmul(bk.ap, lhsT=ot.ap[:, c, ttl * 128:(ttl + 1) * 128],
                                         rhs=w_o.ap[:, c, dh * 512:(dh + 1) * 512], start=(c == 0), stop=(c == 7))
                cx.op("dve", lambda v: v.tensor_tensor(out=xx[ttl].ap[:, dh * 512:(dh + 1) * 512], in0=bk.ap,
                                                       in1=xb.ap[:, dh * 512:(dh + 1) * 512], op=ALU.add),
                      reads=[bk, xb], writes=[xx[ttl]])

            def nrm(ttl):
                rms_to_bf(xx[ttl], xx[ttl].ap, g_ffn, hbf2[ttl % 2])

            def trH(ttl):
                hb = hbf2[ttl % 2]
                transpose_to(hb, hb.ap, tbk[ttl % 2], h2, h2.ap[:, :, ttl * 128:(ttl + 1) * 128])

            pre = [[(gn, 0), (xload, 0)], [(gn, 1), (xload, 1)], [(gn, 2)], [(gn, 3)], [(trO, 0)]] if do_gn else \
                [[(xload, 0)], [(trO, 0), (xload, 1)]]
            steps = pre + [
                [(trO, 1)], [(trO, 2)], [(trO, 3)],
                [(oproj, 0, 0)], [(oproj, 0, 1), (nrm, 0)],
                [(oproj, 1, 0), (xload, 2)], [(oproj, 1, 1), (nrm, 1), (trH, 0)],
                [(oproj, 2, 0), (xload, 3)], [(oproj, 2, 1), (nrm, 2), (trH, 1)],
                [(oproj, 3, 0)], [(oproj, 3, 1), (nrm, 3), (trH, 2)], [], [(trH, 3)],
            ]
            for st_ in steps:
                for f_ in st_:
                    f_[0](*f_[1:])
                yield

        cx.dma("sp", w_o.ap, wo_b.rearrange("(c p) e -> p c e", p=128), reads=[wo_bB], writes=[w_o])
        cx.dma("sp", g_ffn.ap, g_ffn_d.broadcast_to([128, D]), writes=[g_ffn])
        cx.dma("sp", g_mlaC.ap, g_mla_d.broadcast_to([128, 512]), writes=[g_mlaC])
        cx.dma("sp", g_foxC.ap, g_fox_d.broadcast_to([128, 512]), writes=[g_foxC])
        drain(mla_proj(0, 0))
        mscale = 96.0 ** -0.5
        for h in range(8):
            if h < 7:
                attention(QT[h % 2], QTa[h % 2], KT[h % 2], KTa[h % 2], 96, h, mscale, 512 + h * 64,
                          mla_proj(h + 1, (h + 1) % 2))
            else:
                attention(QT[h % 2], QTa[h % 2], KT[h % 2], KTa[h % 2], 96, h, mscale, 512 + h * 64,
                          stageA(0, 0, obk=(PS[5], PS[6]), tbk=(PS[7], PS[7]), do_gn=True), filler_start=4, filler_rate=1)
            run_deferred(2)

        run_deferred(100)
        cx.dma("sp", g_fin.ap, g_fin_d.broadcast_to([128, D]), writes=[g_fin])

        next_wq = None
        pending_final = []
        for n in range(NCH):
            p = n % 2
            h2, xx = h2T[p], x1[p]
            wq_slots = next_wq if next_wq else [load_wgu(i_) for i_ in range(NWGU - 1)]
            next_wq = None
            wd_reqs = [(dh_, f2_) for dh_ in range(2) for f2_ in range(NFC // 2)]
            wd_loaded = []

            def issue_wd():
                if len(wd_loaded) < len(wd_reqs):
                    dh_, f2_ = wd_reqs[len(wd_loaded)]
                    wd_loaded.append(load_wd(dh_, f2_))
            for fc in range(NFC):
                slot = wq_slots.pop(0)
                if fc + NWGU - 1 < NFC:
                    wq_slots.append(load_wgu(fc + NWGU - 1))
                if 2 <= fc < 2 + NWD:
                    issue_wd()
                if fc in (2, 4) and pending_final:
                    pending_final.pop(0)()
                    pending_final.pop(0)()
                if fc == 10 and n + 1 == NCH and s + 1 < nseq:
                    nxt_norm = norm1_gen(s + 1, (PS[0], PS[1]))
                    next(nxt_norm)
                if fc in (6, 12) and n + 1 < NCH:
                    for ttl_ in ((0, 1) if fc == 6 else (2, 3)):
                        group_norm_tile(0, 4 * (n + 1) + ttl_, g_foxC)
                        group_norm_tile(1, 4 * (n + 1) + ttl_, g_mlaC)
                bg = PS[(fc % 2) * 2]
                bu = PS[(fc % 2) * 2 + 1]
                for (bk, gi) in ((bg, 0), (bu, 1)):
                    with cx.grp("pe", reads=[slot, h2], writes=[bk]) as hh:
                        for c in range(8):
                            hh[0] = T.matmul(bk.ap, lhsT=slot.ap[:, gi, c, :], rhs=h2.ap[:, c, :],
                                             start=(c == 0), stop=(c == 7))
                sgb = sg[fc % 2]
                cx.op("act", lambda a: a.activation(out=sgb.ap, in_=bg.ap, func=AF.Silu), reads=[bg], writes=[sgb])
                cx.op("dve", lambda v: v.tensor_tensor(out=aT[fc].ap, in0=sgb.ap, in1=bu.ap, op=ALU.mult),
                      reads=[sgb, bu], writes=[aT[fc]])
            if n + 1 < NCH:
                next_wq = [load_wgu(i_) for i_ in range(NWGU - 1)]
            if n + 1 < NCH:
                nxt = stageA(n + 1, 1 - p)
            elif s + 1 < nseq:
                nxt = nxt_norm
            else:
                nxt = None
            for dh in range(2):
                dbank = [PS[4], PS[5], PS[6], PS[7]]
                for f2 in range(NFC // 2):
                    slot = wd_loaded[dh * (NFC // 2) + f2]
                    if f2 in (0, NFC // 2 - 1):
                        for ttl in range(4):
                            with cx.grp("pe", reads=[slot, aT[2 * f2], aT[2 * f2 + 1]], writes=[dbank[ttl]]) as hh:
                                for a_ in range(2):
                                    fc = 2 * f2 + a_
                                    hh[0] = T.matmul(dbank[ttl].ap, lhsT=aT[fc].ap[:, ttl * 128:(ttl + 1) * 128],
                                                     rhs=slot.ap[:, a_, :], start=(fc == 0), stop=(fc == NFC - 1))
                    else:
                        with cx.grp("pe", reads=[slot, aT[2 * f2], aT[2 * f2 + 1]], writes=[]) as hh:
                            for a_ in range(2):
                                fc = 2 * f2 + a_
                                for ttl in range(4):
                                    hh[0] = T.matmul(dbank[ttl].ap, lhsT=aT[fc].ap[:, ttl * 128:(ttl + 1) * 128],
                                                     rhs=slot.ap[:, a_, :], start=(fc == 0), stop=(fc == NFC - 1))
                    issue_wd()
                    if f2 == NFC // 2 - 1:
                        for ttl in range(4):
                            cx.op("dve", lambda v: v.tensor_tensor(out=xx[ttl].ap[:, dh * 512:(dh + 1) * 512],
                                                                   in0=dbank[ttl].ap,
                                                                   in1=xx[ttl].ap[:, dh * 512:(dh + 1) * 512],
                                                                   op=ALU.add),
                                  reads=[dbank[ttl], xx[ttl]], writes=[xx[ttl]])
                    if nxt is not None:
                        try:
                            next(nxt)
                        except StopIteration:
                            nxt = None
            if nxt is not None:
                drain(nxt)
            def final_tile(ttl, n=n, xx=xx):
                tt = 4 * n + ttl
                ob = outt[ttl % 2]
                st = new_stat()
                cx.op("act", lambda a: a.activation(out=ob.ap, in_=xx[ttl].ap, func=AF.Square,
                                                    accum_out=st.ap[:, 0:1]), reads=[xx[ttl]], writes=[ob, st])
                rstd_from(st, 1024)
                cx.op("dve", lambda v: v.scalar_tensor_tensor(out=ob.ap, in0=xx[ttl].ap, scalar=st.ap[:, 1:2],
                                                              in1=g_fin.ap, op0=ALU.mult, op1=ALU.mult),
                      reads=[xx[ttl], st, g_fin], writes=[ob])
                cx.dma("pool", out[s, tt * 128:(tt + 1) * 128, :], ob.ap, reads=[ob], writes=[], semof=ob)

            if n + 1 < NCH:
                pending_final = [(lambda t=t_, f=final_tile: f(t)) for t_ in range(4)]
            else:
                for t_ in range(4):
                    final_tile(t_)

    for ob in outt:
        for k, t in ob.rds.items():
            cx.wait("pool", t)
    print(f"[kernel] SBUF arena: A={arenaBC - base0} B={endB - arenaBC} C={endC - arenaBC} top={top - base0}")
    return nc


def _freq_tab():
    inv = (np.float32(10000.0) ** (-(np.arange(0, 32, 2, dtype=np.float32)) / np.float32(32))).astype(np.float32)
    tab = np.zeros((128, 4), np.float32)
    tab[64:80, 0] = inv
    tab[80:96, 0] = inv
    tab[64:96, 1] = np.float32(math.pi / 2)
    tab[64:80, 2] = -inv
    tab[80:96, 2] = inv
    return tab


def _in_map(xs, ps, inputs):
    f = lambda a: np.ascontiguousarray(np.asarray(a, dtype=np.float32))
    return {
        "x": f(xs), "positions": np.ascontiguousarray(np.asarray(ps, dtype=np.int32)),
        "norm_mix_g": f(inputs["norm_mix_g"]).reshape(1, D),
        "w_in": f(inputs["w_in"]).reshape(D, 1960),
        "b_fgate": f(inputs["b_fgate"]).reshape(8, 1),
        "q_norm_g": f(inputs["q_norm_g"]).reshape(256, 1),
        "w_uq": f(inputs["w_uq"]).reshape(256, 768),
        "kv_norm_g": f(inputs["kv_norm_g"]).reshape(128, 1),
        "w_ukv": f(inputs["w_ukv"]).reshape(128, 1024),
        "fox_out_g": f(inputs["fox_out_g"]).reshape(1, 512),
        "mla_out_g": f(inputs["mla_out_g"]).reshape(1, 512),
        "w_o": f(inputs["w_o"]).reshape(D, D),
        "norm_ffn_g": f(inputs["norm_ffn_g"]).reshape(1, D),
        "w_gate": f(inputs["w_gate"]).reshape(D, DFF),
        "w_up": f(inputs["w_up"]).reshape(D, DFF),
        "w_down": f(inputs["w_down"]).reshape(DFF, D),
        "final_norm_g": f(inputs["final_norm_g"]).reshape(1, D),
        "freq_tab": _freq_tab(),
    }


def kernel(**inputs):
    ncores = 8
    x = np.asarray(inputs["x"])
    p = np.asarray(inputs["positions"])
    nseq = x.shape[0] // ncores
    nc = build_nc(nseq)
    in_maps = [_in_map(x[i * nseq:(i + 1) * nseq], p[i * nseq:(i + 1) * nseq], inputs) for i in range(ncores)]
    res = run_bass_kernel_spmd(nc, in_maps, core_ids=list(range(ncores)))
    return np.concatenate([np.asarray(r["out"]) for r in res.results], axis=0).astype(np.float32)
```
